# Optimizing a Trainium2 kernel written in Bass

```python
import math
import jax, jax.numpy as jnp
from jax import lax
import numpy as np

D_MODEL = 1024
BATCH = 16
SEQ = 2048
DEPTH = 2

N_EVEN = (DEPTH + 1) // 2
N_ODD = DEPTH // 2
EPS = 1e-6
RET_HEADS = 4
RET_HEAD_DIM = D_MODEL // 8
RET_WIDTH = RET_HEADS * RET_HEAD_DIM
RET_CHUNK = 128
ROPE_BASE = 10000.0
LRU_WIDTH = D_MODEL // 2
LRU_BLOCKS = 4
LRU_BLOCK_DIM = LRU_WIDTH // LRU_BLOCKS
CONV_WIDTH = 4
LRU_C = 8.0
IN_EVEN_WIDTH = 4 * RET_WIDTH + 2 * LRU_WIDTH
S5_GROUP = 16
S5_GROUPS = D_MODEL // S5_GROUP
S5_STATE = 64
S5_CHUNK = 128
DT_MIN = 0.001
DT_MAX = 0.1
D_FF = 2816

kernel_name = "hybrid_retention_rglru_s5_macaron"


def rmsnorm(x, g):
    xf = x.astype(jnp.float32)
    y = xf * lax.rsqrt(jnp.mean(xf * xf, axis=-1, keepdims=True) + EPS)
    return (y * g.astype(jnp.float32)).astype(x.dtype)


def swiglu(x, w1, w3, w2):
    return (jax.nn.silu(x @ w1) * (x @ w3)) @ w2


def rope(x):
    S, Dh = x.shape[1], x.shape[-1]
    half = Dh // 2
    inv = ROPE_BASE ** (-jnp.arange(half, dtype=jnp.float32) / half)
    ang = jnp.arange(S, dtype=jnp.float32)[:, None] * inv[None, :]
    cos = jnp.cos(ang)[None, :, None, :]
    sin = jnp.sin(ang)[None, :, None, :]
    x1, x2 = x[..., :half], x[..., half:]
    return jnp.concatenate([x1 * cos - x2 * sin, x1 * sin + x2 * cos], axis=-1)


def retention_chunkwise(q, k, v):
    B_, S, H, Dh = q.shape
    C = RET_CHUNK
    N = S // C
    q = rope(q)
    k = rope(k) * (Dh ** -0.5)
    log_gamma = jnp.log1p(-jnp.power(2.0, -5.0 - jnp.arange(H, dtype=jnp.float32)))
    pos = jnp.arange(C, dtype=jnp.float32)
    diff = pos[:, None] - pos[None, :]
    decay = jnp.where(diff >= 0, jnp.exp(log_gamma[:, None, None] * jnp.maximum(diff, 0.0)), 0.0)
    qc = q.reshape(B_, N, C, H, Dh)
    kc = k.reshape(B_, N, C, H, Dh)
    vc = v.reshape(B_, N, C, H, Dh)
    scores = jnp.einsum('bnihd,bnjhd->bhnij', qc, kc) * decay[None, :, None, :, :]
    intra = jnp.einsum('bhnij,bnjhd->bnihd', scores, vc)
    k_decay = jnp.exp(log_gamma[:, None] * (C - 1.0 - pos)[None, :])
    kv = jnp.einsum('bnjhd,hj,bnjhe->nbhde', kc, k_decay, vc)
    chunk_decay = jnp.exp(log_gamma * C)[None, :, None, None]

    def step(R, kv_n):
        return R * chunk_decay + kv_n, R

    _, R_prev = lax.scan(step, jnp.zeros((B_, H, Dh, Dh), jnp.float32), kv)
    q_decay = jnp.exp(log_gamma[:, None] * (pos + 1.0)[None, :])
    cross = jnp.einsum('bnihd,hi,nbhde->bnihe', qc, q_decay, R_prev)
    return (intra + cross).reshape(B_, S, H, Dh)


def head_layernorm(x, g):
    mu = jnp.mean(x, axis=-1, keepdims=True)
    var = jnp.mean(jnp.square(x - mu), axis=-1, keepdims=True)
    y = (x - mu) * lax.rsqrt(var + EPS)
    B_, S, H, Dh = x.shape
    return y.reshape(B_, S, H * Dh) * g.astype(jnp.float32)


def causal_depthwise_conv(x, w, b):
    K = w.shape[0]
    S = x.shape[1]
    xp = jnp.pad(x, ((0, 0), (K - 1, 0), (0, 0)))
    out = b
    for tap in range(K):
        out = out + xp[:, tap:tap + S, :] * w[tap]
    return out


def rg_lru(x, w_a, b_a, w_i, b_i, lam):
    B_, S, W = x.shape
    xb = x.reshape(B_, S, LRU_BLOCKS, LRU_BLOCK_DIM)
    r = jax.nn.sigmoid(jnp.einsum('bsgi,gio->bsgo', xb, w_a.astype(jnp.float32)).reshape(B_, S, W) + b_a.astype(jnp.float32))
    i = jax.nn.sigmoid(jnp.einsum('bsgi,gio->bsgo', xb, w_i.astype(jnp.float32)).reshape(B_, S, W) + b_i.astype(jnp.float32))
    log_a = -LRU_C * r * jax.nn.softplus(-lam.astype(jnp.float32))
    a = jnp.exp(log_a)
    mult = jnp.sqrt(-jnp.expm1(2.0 * log_a))
    bx = mult * i * x

    def comb(c1, c2):
        a1, b1 = c1
        a2, b2 = c2
        return a1 * a2, a2 * b1 + b2

    _, h = lax.associative_scan(comb, (a, bx), axis=1)
    return h


def s5_ssm(u, lam_re, lam_im, log_dt, b_re, b_im, c_re, c_im, d):
    B_, S, H = u.shape
    G, P = lam_re.shape
    L = S5_CHUNK
    N = S // L
    lam_re = lam_re.astype(jnp.float32)
    lam_im = lam_im.astype(jnp.float32)
    dt = jnp.exp(log_dt.astype(jnp.float32))[:, None]
    mag = jnp.exp(lam_re * dt)
    lbar_re = mag * jnp.cos(lam_im * dt)
    lbar_im = mag * jnp.sin(lam_im * dt)
    den = lam_re * lam_re + lam_im * lam_im
    nr = lbar_re - 1.0
    ni = lbar_im
    f_re = ((nr * lam_re + ni * lam_im) / den)[..., None]
    f_im = ((ni * lam_re - nr * lam_im) / den)[..., None]
    b_re = b_re.astype(jnp.float32)
    b_im = b_im.astype(jnp.float32)
    bbar_re = f_re * b_re - f_im * b_im
    bbar_im = f_re * b_im + f_im * b_re
    c_re = c_re.astype(jnp.float32)
    c_im = c_im.astype(jnp.float32)
    a_re = jnp.broadcast_to(lbar_re, (B_, L, G, P))
    a_im = jnp.broadcast_to(lbar_im, (B_, L, G, P))
    uc = jnp.swapaxes(u.reshape(B_, N, L, G, S5_GROUP), 0, 1)

    def comb(c1, c2):
        ar1, ai1, br1, bi1 = c1
        ar2, ai2, br2, bi2 = c2
        return (ar1 * ar2 - ai1 * ai2,
                ar1 * ai2 + ai1 * ar2,
                ar2 * br1 - ai2 * bi1 + br2,
                ar2 * bi1 + ai2 * br1 + bi2)

    def chunk_step(h0, u_n):
        h0_re, h0_im = h0
        bu_re = jnp.einsum('blgc,gpc->blgp', u_n, bbar_re)
        bu_im = jnp.einsum('blgc,gpc->blgp', u_n, bbar_im)
        p_re, p_im, hl_re, hl_im = lax.associative_scan(comb, (a_re, a_im, bu_re, bu_im), axis=1)
        h_re = hl_re + p_re * h0_re[:, None] - p_im * h0_im[:, None]
        h_im = hl_im + p_re * h0_im[:, None] + p_im * h0_re[:, None]
        y = jnp.einsum('blgp,gcp->blgc', h_re, c_re) - jnp.einsum('blgp,gcp->blgc', h_im, c_im)
        return (h_re[:, -1], h_im[:, -1]), y

    h_init = (jnp.zeros((B_, G, P), jnp.float32), jnp.zeros((B_, G, P), jnp.float32))
    _, y = lax.scan(chunk_step, h_init, uc)
    y = jnp.swapaxes(y, 0, 1).reshape(B_, S, H)
    return y + d.astype(jnp.float32) * u


def even_mixer(h, w_in, w_out, ret_norm_g, conv_w, conv_b, w_a, b_a, w_i, b_i, lam):
    B_, S, _ = h.shape
    proj = (h @ w_in).astype(jnp.float32)
    q, k, v, g_ret, x_lru, g_lru = jnp.split(
        proj, [RET_WIDTH, 2 * RET_WIDTH, 3 * RET_WIDTH, 4 * RET_WIDTH, 4 * RET_WIDTH + LRU_WIDTH], axis=-1)
    shp = (B_, S, RET_HEADS, RET_HEAD_DIM)
    ret = retention_chunkwise(q.reshape(shp), k.reshape(shp), v.reshape(shp))
    ret = head_layernorm(ret, ret_norm_g) * jax.nn.silu(g_ret)
    xc = causal_depthwise_conv(x_lru, conv_w.astype(jnp.float32), conv_b.astype(jnp.float32))
    lru = rg_lru(xc, w_a, b_a, w_i, b_i, lam) * jax.nn.gelu(g_lru)
    merged = jnp.concatenate([ret, lru], axis=-1).astype(h.dtype)
    return merged @ w_out


def odd_mixer(h, lam_re, lam_im, log_dt, b_re, b_im, c_re, c_im, d, glu_w_a, glu_w_b):
    y = s5_ssm(h.astype(jnp.float32), lam_re, lam_im, log_dt, b_re, b_im, c_re, c_im, d)
    y = jax.nn.gelu(y).astype(h.dtype)
    return (y @ glu_w_a) * jax.nn.sigmoid(y @ glu_w_b)


def setup_inputs(seed: int = 0) -> dict:
    key = jax.random.key(seed)
    ks = jax.random.split(key, 32)
    f32 = jnp.float32

    def nrm(k, shape, scale):
        return jax.random.normal(k, shape, f32) * scale

    x = nrm(ks[0], (BATCH, SEQ, D_MODEL), 1.0)
    ffn_norm_g = 1.0 + nrm(ks[1], (DEPTH, 2, D_MODEL), 0.02)
    ffn_w1 = nrm(ks[2], (DEPTH, 2, D_MODEL, D_FF), D_MODEL ** -0.5)
    ffn_w3 = nrm(ks[3], (DEPTH, 2, D_MODEL, D_FF), D_MODEL ** -0.5)
    ffn_w2 = nrm(ks[4], (DEPTH, 2, D_FF, D_MODEL), D_FF ** -0.5)
    mix_norm_g = 1.0 + nrm(ks[5], (DEPTH, D_MODEL), 0.02)
    w_in_even = nrm(ks[6], (N_EVEN, D_MODEL, IN_EVEN_WIDTH), D_MODEL ** -0.5)
    w_out_even = nrm(ks[7], (N_EVEN, RET_WIDTH + LRU_WIDTH, D_MODEL), (RET_WIDTH + LRU_WIDTH) ** -0.5)
    ret_norm_g = 1.0 + nrm(ks[8], (N_EVEN, RET_WIDTH), 0.02)
    conv_w = nrm(ks[9], (N_EVEN, CONV_WIDTH, LRU_WIDTH), CONV_WIDTH ** -0.5)
    conv_b = nrm(ks[10], (N_EVEN, LRU_WIDTH), 0.01)
    lru_w_a = nrm(ks[11], (N_EVEN, LRU_BLOCKS, LRU_BLOCK_DIM, LRU_BLOCK_DIM), LRU_BLOCK_DIM ** -0.5)
    lru_b_a = nrm(ks[12], (N_EVEN, LRU_WIDTH), 0.01)
    lru_w_i = nrm(ks[13], (N_EVEN, LRU_BLOCKS, LRU_BLOCK_DIM, LRU_BLOCK_DIM), LRU_BLOCK_DIM ** -0.5)
    lru_b_i = nrm(ks[14], (N_EVEN, LRU_WIDTH), 0.01)
    a_c = jax.random.uniform(ks[15], (N_EVEN, LRU_WIDTH), f32, 0.9, 0.999)
    s = a_c ** (1.0 / LRU_C)
    lru_lambda = jnp.log(s) - jnp.log1p(-s)
    n = jnp.arange(S5_STATE, dtype=f32)
    s5_lambda_re = -0.5 + nrm(ks[16], (N_ODD, S5_GROUPS, S5_STATE), 0.01)
    s5_lambda_im = math.pi * n + nrm(ks[17], (N_ODD, S5_GROUPS, S5_STATE), 0.01)
    s5_log_dt = jax.random.uniform(ks[18], (N_ODD, S5_GROUPS), f32, math.log(DT_MIN), math.log(DT_MAX))
    s5_b_re = nrm(ks[19], (N_ODD, S5_GROUPS, S5_STATE, S5_GROUP), (2 * S5_GROUP) ** -0.5)
    s5_b_im = nrm(ks[20], (N_ODD, S5_GROUPS, S5_STATE, S5_GROUP), (2 * S5_GROUP) ** -0.5)
    s5_c_re = nrm(ks[21], (N_ODD, S5_GROUPS, S5_GROUP, S5_STATE), (2 * S5_STATE) ** -0.5)
    s5_c_im = nrm(ks[22], (N_ODD, S5_GROUPS, S5_GROUP, S5_STATE), (2 * S5_STATE) ** -0.5)
    s5_d = nrm(ks[23], (N_ODD, D_MODEL), 1.0)
    glu_w_a = nrm(ks[24], (N_ODD, D_MODEL, D_MODEL), D_MODEL ** -0.5)
    glu_w_b = nrm(ks[25], (N_ODD, D_MODEL, D_MODEL), D_MODEL ** -0.5)
    final_norm_g = 1.0 + nrm(ks[26], (D_MODEL,), 0.02)
    return {"x": x, "ffn_norm_g": ffn_norm_g, "ffn_w1": ffn_w1, "ffn_w3": ffn_w3, "ffn_w2": ffn_w2,
            "mix_norm_g": mix_norm_g, "w_in_even": w_in_even, "w_out_even": w_out_even,
            "ret_norm_g": ret_norm_g, "conv_w": conv_w, "conv_b": conv_b,
            "lru_w_a": lru_w_a, "lru_b_a": lru_b_a, "lru_w_i": lru_w_i, "lru_b_i": lru_b_i,
            "lru_lambda": lru_lambda, "s5_lambda_re": s5_lambda_re, "s5_lambda_im": s5_lambda_im,
            "s5_log_dt": s5_log_dt, "s5_b_re": s5_b_re, "s5_b_im": s5_b_im,
            "s5_c_re": s5_c_re, "s5_c_im": s5_c_im, "s5_d": s5_d,
            "glu_w_a": glu_w_a, "glu_w_b": glu_w_b, "final_norm_g": final_norm_g}


def reference(x, ffn_norm_g, ffn_w1, ffn_w3, ffn_w2, mix_norm_g, w_in_even, w_out_even,
              ret_norm_g, conv_w, conv_b, lru_w_a, lru_b_a, lru_w_i, lru_b_i, lru_lambda,
              s5_lambda_re, s5_lambda_im, s5_log_dt, s5_b_re, s5_b_im, s5_c_re, s5_c_im, s5_d,
              glu_w_a, glu_w_b, final_norm_g):
    for layer in range(DEPTH):
        x = x + 0.5 * swiglu(rmsnorm(x, ffn_norm_g[layer, 0]), ffn_w1[layer, 0], ffn_w3[layer, 0], ffn_w2[layer, 0])
        h = rmsnorm(x, mix_norm_g[layer])
        if layer % 2 == 0:
            e = layer // 2
            x = x + even_mixer(h, w_in_even[e], w_out_even[e], ret_norm_g[e], conv_w[e], conv_b[e],
                               lru_w_a[e], lru_b_a[e], lru_w_i[e], lru_b_i[e], lru_lambda[e])
        else:
            o = layer // 2
            x = x + odd_mixer(h, s5_lambda_re[o], s5_lambda_im[o], s5_log_dt[o], s5_b_re[o], s5_b_im[o],
                              s5_c_re[o], s5_c_im[o], s5_d[o], glu_w_a[o], glu_w_b[o])
        x = x + 0.5 * swiglu(rmsnorm(x, ffn_norm_g[layer, 1]), ffn_w1[layer, 1], ffn_w3[layer, 1], ffn_w2[layer, 1])
    return rmsnorm(x, final_norm_g)
```

```python
import math
from contextlib import ExitStack
import numpy as np
import concourse.bass as bass
import concourse.mybir as mybir
from concourse.bass_utils import run_bass_kernel_spmd

F32 = mybir.dt.float32
BF16 = mybir.dt.bfloat16
ALU = mybir.AluOpType
AF = mybir.ActivationFunctionType
AX = mybir.AxisListType

D = 1024
S = 2048
NCK = 8
DFF = 2816
NFC = 22
TT = 512
NTT = S // TT
EPS = 1e-6
NSEQ = 2
EPOCH = 16000
FFN_G = 2
HOIST_NORM = False

ENGS = ["tensor", "vector", "scalar", "gpsimd", "sync"]


class Op:
    __slots__ = ("eng", "idx", "fn", "dma", "deps", "sig", "ticket", "dma_cnt")

    def __init__(self, eng, idx, fn, dma):
        self.eng = eng
        self.idx = idx
        self.fn = fn
        self.dma = dma
        self.deps = ()
        self.sig = False
        self.ticket = None
        self.dma_cnt = None


class Prog:
    def __init__(self, nc, es):
        self.nc = nc
        self.es = es
        self.ops = {e: [] for e in ENGS}
        self.lastw = {}
        self.readers = {}
        self.dma_count = {}
        self.same_engine_sync = True
        self.bar = set()
        self.last_dma = {}

    def barrier(self):
        b = set()
        for e in ENGS:
            if self.ops[e]:
                lo = self.ops[e][-1]
                if lo.dma is None:
                    b.add(lo)
                else:
                    for o in reversed(self.ops[e]):
                        if o.dma is None:
                            b.add(o)
                            break
        for k, o in self.last_dma.items():
            b.add(o)
        self.bar = b

    def op(self, eng, fn, r=(), w=(), dma=None):
        o = Op(eng, len(self.ops[eng]), fn, dma)
        deps = set()
        for res in r:
            lw = self.lastw.get(res)
            if lw is not None:
                deps.add(lw)
        for res in w:
            lw = self.lastw.get(res)
            if lw is not None:
                deps.add(lw)
            for rd in self.readers.get(res, ()):
                deps.add(rd)
        deps |= self.bar
        deps.discard(o)
        o.deps = deps
        if dma is not None:
            self.last_dma[dma] = o
        for res in w:
            self.lastw[res] = o
            self.readers[res] = []
        for res in r:
            if res not in w:
                self.readers.setdefault(res, []).append(o)
        if dma is not None:
            self.dma_count[dma] = self.dma_count.get(dma, 0) + 1
            o.dma_cnt = self.dma_count[dma]
        self.ops[eng].append(o)
        return o

    def emit(self, final_dma_keys):
        nc = self.nc
        for e in ENGS:
            for o in self.ops[e]:
                for d in o.deps:
                    if d.dma is None:
                        if d.eng == o.eng and (d.eng == "tensor" or not self.same_engine_sync):
                            continue
                        d.sig = True
        nep = {}
        for e in ENGS:
            c = 0
            for o in self.ops[e]:
                if o.sig:
                    o.ticket = (c // EPOCH, c % EPOCH + 1)
                    c += 1
            nep[e] = c // EPOCH + 1
        sems = {}
        for e in ENGS:
            for k in range(nep[e]):
                sems[(e, k)] = self.es.enter_context(nc.semaphore(f"s_{e}_{k}"))
        dsems = {}
        for key in self.dma_count:
            dsems[key] = self.es.enter_context(nc.semaphore(f"d_{key}"))
        block = self.es.enter_context(nc.Block())

        def run(engname, eng):
            waited = {}
            for o in self.ops[engname]:
                need = {}
                for d in o.deps:
                    if d.dma is not None:
                        k = ("dma", d.dma)
                        v = 16 * d.dma_cnt
                    else:
                        if d.eng == engname and (engname == "tensor" or not self.same_engine_sync):
                            continue
                        k = (d.eng, d.ticket[0])
                        v = d.ticket[1]
                    if v > need.get(k, 0):
                        need[k] = v
                for k, v in need.items():
                    if k[0] != "dma":
                        newer = [kk for kk in waited if kk[0] == k[0] and kk[0] != "dma" and kk[1] > k[1]]
                        if newer:
                            continue
                    if waited.get(k, 0) >= v:
                        continue
                    waited[k] = v
                    sem = dsems[k[1]] if k[0] == "dma" else sems[k]
                    eng.wait_ge(sem, v)
                ins = o.fn(eng)
                if o.dma is not None:
                    ins.then_inc(dsems[o.dma], 16)
                elif o.sig:
                    ins.then_inc(sems[(engname, o.ticket[0])], 1)
            if engname == "sync":
                for key in final_dma_keys:
                    eng.wait_ge(dsems[key], 16 * self.dma_count[key])

        @block.tensor
        def _(t):
            run("tensor", t)

        @block.vector
        def _(v):
            run("vector", v)

        @block.scalar
        def _(a):
            run("scalar", a)

        @block.gpsimd
        def _(g):
            run("gpsimd", g)

        @block.sync
        def _(s):
            run("sync", s)


class Builder:
    def __init__(self, stages=None, nseq=NSEQ, ffn_g=FFN_G):
        self.stages = stages
        self.nseq = nseq
        self.G = ffn_g
        self.es = ExitStack()
        nc = bass.Bass("TRN2", target_bir_lowering=False)
        self.nc = nc
        self.P = Prog(nc, self.es)
        es = self.es

        def din(name, shape):
            return nc.dram_tensor(name, list(shape), F32, kind="ExternalInput").ap()

        self.xT = din("xT", [nseq, D, S])
        self.outT = nc.dram_tensor("outT", [nseq, D, S], F32, kind="ExternalOutput").ap()
        G0 = ffn_g
        assert NFC % G0 == 0
        self.w1 = din("ffn_w1", [2, 2, NFC // G0, 128, NCK, G0 * 128])
        self.w3 = din("ffn_w3", [2, 2, NFC // G0, 128, NCK, G0 * 128])
        self.w2 = din("ffn_w2", [2, 2, NFC // G0, 128, G0, D])
        self.gains_d = din("gains", [128, 7 * NCK])
        self.d_cos = din("rope_cos", [128, S])
        self.d_sin = din("rope_sin", [128, S])
        self.d_mask = din("ret_mask", [128, 4, 128])
        self.d_gret = din("gret", [128, 4, 128])
        self.d_small = din("even_small", [128, 64])
        self.d_ident = din("ident", [128, 128])
        self.d_wg = din("lru_wg", [128, 2, 4, 128])
        self.d_wl = din("w_lru", [4, D, 256])
        self.d_wh = din("w_head", [4, D, 768])
        self.d_wout = din("w_out_even", [D, D])
        self.d_iota = din("iota", [128, 260])
        self.d_pm = din("parmask", [128, 4])
        self.d_s5pp = din("s5_pp", [128, 3, 32])
        self.d_s5bc = din("s5_bc", [128, 4, 32, 16])
        self.d_s5d = din("s5_d8", [128, 8])
        self.d_glu = din("glu_w", [2, D, D])

        def sb(name, shape, dt):
            return es.enter_context(nc.sbuf_tensor(name, list(shape), dt))

        self.X = sb("X", [128, NCK, S], F32)
        self.XN = sb("XN", [128, NCK, S], BF16)
        self.gains = sb("gains_sb", [128, 7 * NCK], F32)
        self.ones = sb("ones", [128, 128], BF16)
        self.negpi = sb("negpi", [128, 2], F32)
        AW = 28160
        self.ARENA = sb("ARENA", [128, AW], F32)
        self.AW = AW
        G = self.G
        self.SQ = self.carve(0, [NCK, TT], BF16)
        self.RS = self.carve(2048, [2, TT], F32)
        o = 3072
        self.W1S = self.carve(o, [2, NCK, G * 128], BF16); o += NCK * G * 128
        self.W3S = self.carve(o, [2, NCK, G * 128], BF16); o += NCK * G * 128
        self.W2S = self.carve(o, [2, G, D], BF16); o += G * D
        self.SG = self.carve(o, [2, TT], F32); o += 2 * TT
        self.GT = self.carve(o, [2, G, TT], BF16); o += G * TT
        assert o <= AW
        self.ps = [es.enter_context(nc.psum_tensor(f"ps{i}", [128, 512], F32)) for i in range(8)]
        self.rr = {}
        self.pending_down = None
        self.lru_pending = None
        self.next_norm = None
        self.norm_done = False
        self.even_prefetched = False

    def carve(self, off, shape, dt):
        n = 1
        for d in shape:
            n *= d
        words = n if dt == F32 else (n + 1) // 2
        assert off + words <= self.AW, (off, words)
        v = self.ARENA[:, off:off + words]
        if dt != F32:
            v = v.bitcast(dt)
        if len(shape) == 2:
            v = v.rearrange("p (a b) -> p a b", b=shape[1])
        elif len(shape) == 3:
            v = v.rearrange("p (a b c) -> p a b c", b=shape[1], c=shape[2])
        elif len(shape) == 4:
            v = v.rearrange("p (a b c d) -> p a b c d", b=shape[1], c=shape[2], d=shape[3])
        return v

    def dbg(self, name, ap, shape, dt, rd):
        if not getattr(self, "debug", False):
            return
        d = self.nc.dram_tensor("dbg_" + name, list(shape), dt, kind="ExternalOutput").ap()
        self.P.op("sync", lambda e: e.dma_start(out=d, in_=ap), r=rd, dma="out")

    def rot(self, key, n):
        v = self.rr.get(key, 0)
        self.rr[key] = v + 1
        return v % n

    def setup(self):
        P = self.P
        P.op("sync", lambda e: e.dma_start(out=self.gains[:], in_=self.gains_d), w=["gains"], dma="gains")
        P.op("vector", lambda e: e.memset(self.ones[:], 1.0 / D), w=["ones"])
        P.op("vector", lambda e: e.memset(self.negpi[:], -math.pi), w=["negpi"])

    def load_x(self, sq):
        P = self.P
        for ck in range(NCK):
            P.op("sync",
                 lambda e, ck=ck: e.dma_start(out=self.X[:, ck, :], in_=self.xT[sq, ck * 128:(ck + 1) * 128, :]),
                 w=[f"X{ck}_{tt}" for tt in range(NTT)], dma=f"x{ck}")

    def rmsnorm(self, gidx, out_bf16=True, rs_all=None, deint=False, tiles=None):
        P = self.P
        for tt in (range(NTT) if tiles is None else tiles):
            ts = slice(tt * TT, (tt + 1) * TT)
            for ck in range(NCK):
                P.op("scalar",
                     lambda e, ck=ck, ts=ts: e.activation(out=self.SQ[:, ck, :], in_=self.X[:, ck, ts], func=AF.Square),
                     r=[f"X{ck}_{tt}"], w=[f"SQ{ck}"])
            pb = 7

            def mm(e, pb=pb):
                for ck in range(NCK):
                    ins = e.matmul(self.ps[pb][:, :], lhsT=self.ones[:, :], rhs=self.SQ[:, ck, :],
                                   start=(ck == 0), stop=(ck == NCK - 1))
                return ins
            P.op("tensor", mm, r=["ones"] + [f"SQ{ck}" for ck in range(NCK)], w=[f"ps{pb}"])
            rs = self.rot("rs", 2)
            P.op("scalar",
                 lambda e, rs=rs, pb=pb: e.activation(out=self.RS[:, rs, :], in_=self.ps[pb][:, :], func=AF.Sqrt,
                                                      bias=EPS, scale=1.0),
                 r=[f"ps{pb}"], w=[f"RS{rs}"])
            P.op("vector", lambda e, rs=rs: e.reciprocal(out=self.RS[:, rs, :], in_=self.RS[:, rs, :]),
                 r=[f"RS{rs}"], w=[f"RS{rs}"])
            if rs_all is not None:
                P.op("gpsimd", lambda e, rs=rs, ts=ts: e.tensor_copy(out=rs_all[:, ts], in_=self.RS[:, rs, :]),
                     r=[f"RS{rs}"], w=["RSALL"])
            for ck in range(NCK):
                if out_bf16 and deint:
                    P.op("vector",
                         lambda e, ck=ck, ts=ts, rs=rs, tt=tt: e.scalar_tensor_tensor(
                             out=self.XN[:, ck, :].rearrange("p (j m) -> p m j", j=8)[:, tt * 64:(tt + 1) * 64, :],
                             in0=self.X[:, ck, ts].rearrange("p (m j) -> p m j", j=8),
                             scalar=self.gains[:, gidx * NCK + ck:gidx * NCK + ck + 1],
                             in1=self.RS[:, rs, :].rearrange("p (m j) -> p m j", j=8), op0=ALU.mult, op1=ALU.mult),
                         r=[f"X{ck}_{tt}", f"RS{rs}", "gains"], w=[f"XN{ck}_{t2}" for t2 in range(NTT)])
                elif out_bf16:
                    P.op("vector",
                         lambda e, ck=ck, ts=ts, rs=rs: e.scalar_tensor_tensor(
                             out=self.XN[:, ck, ts], in0=self.X[:, ck, ts],
                             scalar=self.gains[:, gidx * NCK + ck:gidx * NCK + ck + 1],
                             in1=self.RS[:, rs, :], op0=ALU.mult, op1=ALU.mult),
                         r=[f"X{ck}_{tt}", f"RS{rs}", "gains"], w=[f"XN{ck}_{tt}"])
                else:
                    P.op("vector",
                         lambda e, ck=ck, ts=ts, rs=rs: e.scalar_tensor_tensor(
                             out=self.X[:, ck, ts], in0=self.X[:, ck, ts],
                             scalar=self.gains[:, gidx * NCK + ck:gidx * NCK + ck + 1],
                             in1=self.RS[:, rs, :], op0=ALU.mult, op1=ALU.mult),
                         r=[f"X{ck}_{tt}", f"RS{rs}", "gains"], w=[f"X{ck}_{tt}"])

    def tail_norm(self, tt):
        if self.next_norm is not None:
            self.rmsnorm(tiles=[tt], **self.next_norm)
            if tt == NTT - 1:
                self.norm_done = True

    def ffn(self, l, i):
        P = self.P
        G = self.G
        own_norm = not self.norm_done
        self.norm_done = False
        if own_norm:
            self.rmsnorm(l * 2 + i, tiles=[0])
        c0 = 0
        while c0 < NFC:
            g_n = min(G, NFC - c0)
            s = self.rot("wslot", 2)
            P.op("gpsimd", lambda e, s=s, c0=c0, g_n=g_n: e.dma_start(
                out=self.W1S[:, s, :, :], in_=self.w1[l, i, c0 // G]),
                w=[f"W1S{s}"], dma=f"W1S{s}")
            P.op("gpsimd", lambda e, s=s, c0=c0, g_n=g_n: e.dma_start(
                out=self.W3S[:, s, :, :], in_=self.w3[l, i, c0 // G]),
                w=[f"W3S{s}"], dma=f"W3S{s}")
            P.op("gpsimd", lambda e, s=s, c0=c0, g_n=g_n: e.dma_start(
                out=self.W2S[:, s, :, :], in_=self.w2[l, i, c0 // G]),
                w=[f"W2S{s}"], dma=f"W2S{s}")
            for tt in range(NTT):
                ts = slice(tt * TT, (tt + 1) * TT)
                gs = self.rot("gt", 2)
                if own_norm and c0 == 0 and tt + 1 < NTT:
                    self.rmsnorm(l * 2 + i, tiles=[tt + 1])
                for g in range(g_n):
                    b1 = self.rot("b1", 2)
                    b3 = 2 + self.rot("b3", 2)

                    def up(e, wt, pb, g=g, s=s, ts=ts):
                        for ck in range(NCK):
                            ins = e.matmul(self.ps[pb][:, :], lhsT=wt[:, s, ck, g * 128:(g + 1) * 128],
                                           rhs=self.XN[:, ck, ts], start=(ck == 0), stop=(ck == NCK - 1))
                        return ins
                    xnr = [f"XN{ck}_{tt}" for ck in range(NCK)]
                    P.op("tensor", lambda e, up=up, b1=b1: up(e, self.W1S, b1), r=[f"W1S{s}"] + xnr, w=[f"ps{b1}"])
                    P.op("tensor", lambda e, up=up, b3=b3: up(e, self.W3S, b3), r=[f"W3S{s}"] + xnr, w=[f"ps{b3}"])
                    sg = self.rot("sg", 2)
                    P.op("scalar", lambda e, sg=sg, b1=b1: e.activation(out=self.SG[:, sg, :], in_=self.ps[b1][:, :],
                                                                        func=AF.Silu),
                         r=[f"ps{b1}"], w=[f"SG{sg}"])
                    P.op("vector", lambda e, sg=sg, b3=b3, gs=gs, g=g: e.tensor_tensor(
                        out=self.GT[:, gs, g, :], in0=self.SG[:, sg, :], in1=self.ps[b3][:, :], op=ALU.mult),
                        r=[f"SG{sg}", f"ps{b3}"], w=[f"GT{gs}_{g}"])
                if self.pending_down is not None:
                    self.pending_down()

                last_group = (c0 + g_n >= NFC)

                def emit_down(s=s, gs=gs, g_n=g_n, ts=ts, tt=tt, last_group=last_group):
                    for c in range(NCK):
                        bo = 4 + self.rot("bo", 3)

                        def down(e, c=c, bo=bo):
                            for g in range(g_n):
                                ins = e.matmul(self.ps[bo][:, :], lhsT=self.W2S[:, s, g, c * 128:(c + 1) * 128],
                                               rhs=self.GT[:, gs, g, :], start=(g == 0), stop=(g == g_n - 1))
                            return ins
                        P.op("tensor", down, r=[f"W2S{s}"] + [f"GT{gs}_{g}" for g in range(g_n)], w=[f"ps{bo}"])
                        P.op("vector", lambda e, c=c, bo=bo: e.scalar_tensor_tensor(
                            out=self.X[:, c, ts], in0=self.ps[bo][:, :], scalar=0.5, in1=self.X[:, c, ts],
                            op0=ALU.mult, op1=ALU.add),
                            r=[f"ps{bo}", f"X{c}_{tt}"], w=[f"X{c}_{tt}"])
                    if last_group:
                        self.tail_norm(tt)
                self.pending_down = emit_down
            c0 += g_n
        if self.pending_down is not None:
            self.pending_down()
            self.pending_down = None


    def even_prefetch(self):
        P = self.P
        COS = self.carve(11264, [S], BF16)
        SIN = self.carve(12288, [S], BF16)
        o = 18432
        MASK = self.carve(o, [4, 128], F32); o += 512
        GRET = self.carve(o, [4, 128], F32); o += 512
        SMALL = self.carve(o, [64], F32); o += 64
        o += 16
        IDENT = self.carve(o, [128], BF16); o += 64
        WG = self.carve(o, [2, 4, 128], BF16); o += 512

        def S_(e, out, in_):
            return e.dma_start(out=out, in_=in_)
        P.op("gpsimd", lambda e: S_(e, COS, self.d_cos), w=["COS"], dma="COS")
        P.op("gpsimd", lambda e: S_(e, SIN, self.d_sin), w=["SIN"], dma="SIN")
        P.op("sync", lambda e: S_(e, MASK, self.d_mask), w=["MASK"], dma="MASK")
        P.op("sync", lambda e: S_(e, GRET, self.d_gret), w=["GRET"], dma="GRET")
        P.op("sync", lambda e: S_(e, SMALL, self.d_small), w=["SMALL"], dma="SMALL")
        P.op("gpsimd", lambda e: S_(e, IDENT, self.d_ident), w=["IDENT"], dma="IDENT")
        P.op("gpsimd", lambda e: S_(e, WG, self.d_wg), w=["WG"], dma="WG")
        self.even_prefetched = True

    def even(self):
        P = self.P
        V = lambda fn, r, w: P.op("vector", fn, r=r, w=w)
        A = lambda fn, r, w: P.op("scalar", fn, r=r, w=w)
        T = lambda fn, r, w: P.op("tensor", fn, r=r, w=w)
        GP = lambda fn, r, w: P.op("gpsimd", fn, r=r, w=w)
        P.barrier()
        if not self.norm_done:
            self.rmsnorm(4)
        self.norm_done = False
        P.barrier()
        KT = self.carve(0, [16, 128], BF16)
        RB = self.carve(1024, [16, 128], BF16)
        MERGED = self.carve(3072, [NCK, S], BF16)
        COS = self.carve(11264, [S], BF16)
        SIN = self.carve(12288, [S], BF16)
        WM = self.carve(15360, [NCK, 768], BF16)
        o = 18432
        MASK = self.carve(o, [4, 128], F32); o += 512
        GRET = self.carve(o, [4, 128], F32); o += 512
        SMALL = self.carve(o, [64], F32); o += 64
        C12 = self.carve(o, [16], F32); o += 16
        IDENT = self.carve(o, [128], BF16); o += 64
        WG = self.carve(o, [2, 4, 128], BF16); o += 512
        assert o <= 20288
        o = 20288
        QP2 = self.carve(o, [2, S], BF16); o += 2048
        KP2 = self.carve(o, [2, S], BF16); o += 2048
        VT2 = self.carve(13312, [2, 16, 128], BF16)
        SGR2 = self.carve(o, [2, 16, 128], BF16); o += 2048
        TMP = self.carve(o, [3, 512], F32); o += 1536
        assert o <= self.AW
        o2 = 2048
        R = self.carve(o2, [128], F32); o2 += 128
        AS = self.carve(o2, [2, 128], BF16); o2 += 128
        MT = self.carve(o2, [4, 128], BF16); o2 += 256
        ST = self.carve(o2, [8, 4], F32); o2 += 32
        assert o2 <= 3072
        assert o <= self.AW
        o = 20288
        XL = self.carve(o, [2, 516], F32); o += 1032
        LW = self.carve(o, [9, 512], F32); o += 9 * 512
        LWB = self.carve(0, [6, 512], F32)
        XCB = self.carve(o, [512], BF16); o += 256
        HL = self.carve(o, [2, 512], F32); o += 1024
        assert o <= self.AW
        psb = [p[:, :].bitcast(BF16) for p in self.ps]

        if not self.even_prefetched:
            self.even_prefetch()
        self.even_prefetched = False

        def proj_fm(col0, tt, pb):
            ts = slice(tt * TT, (tt + 1) * TT)

            def f(e):
                for ck in range(NCK):
                    ins = e.matmul(self.ps[pb][:, :], lhsT=WM[:, ck, col0:col0 + 128], rhs=self.XN[:, ck, ts],
                                   start=(ck == 0), stop=(ck == NCK - 1))
                return ins
            T(f, ["WM"] + [f"XN{ck}_{tt}" for ck in range(NCK)], [f"ps{pb}"])

        A(lambda e: e.activation(out=C12[:, 0:4], in_=SMALL[:, 36:40], func=AF.Exp, scale=-1.0), ["SMALL"], ["C12a"])
        A(lambda e: e.activation(out=C12[:, 0:4], in_=C12[:, 0:4], func=AF.Ln, bias=1.0, scale=1.0), ["C12a"], ["C12a"])
        V(lambda e: e.tensor_scalar(out=C12[:, 4:8], in0=C12[:, 0:4], scalar1=-8.0, scalar2=None, op0=ALU.mult),
          ["C12a"], ["C12b"])
        V(lambda e: e.tensor_scalar(out=C12[:, 8:12], in0=C12[:, 0:4], scalar1=-16.0, scalar2=None, op0=ALU.mult),
          ["C12a"], ["C12c"])
        for b in range(4):
            P.op("gpsimd", lambda e, b=b: e.dma_start(out=WM[:, :, 0:256],
                                                     in_=self.d_wl[b].rearrange("(k p) n -> p k n", p=128)),
                 w=["WM"], dma="WM")
            for pair in range(2):
                ctx = []
                for tt in (2 * pair, 2 * pair + 1):
                    ts = slice(tt * TT, (tt + 1) * TT)
                    xs = tt % 2
                    par = tt % 2
                    LWs = [LW[:, k, :] for k in range(7)] if par == 0 else ([LWB[:, k, :] for k in range(6)] + [LW[:, 7, :]])
                    XCBs = XCB[:, :] if par == 0 else LW[:, 8, :].bitcast(BF16)[:, 0:512]
                    nm = lambda base, par=par: f"{base}{par}"
                    pxl = self.rot("psE", 8)
                    proj_fm(0, tt, pxl)
                    pgl = self.rot("psE", 8)
                    proj_fm(128, tt, pgl)
                    if tt == 0:
                        V(lambda e, xs=xs: e.memset(XL[:, xs, 0:4], 0.0), [], [f"XLh{xs}"])
                    A(lambda e, xs=xs, pxl=pxl: e.activation(out=XL[:, xs, 4:516], in_=self.ps[pxl][:, :], func=AF.Copy),
                      [f"ps{pxl}"], [f"XL{xs}"])
                    GP(lambda e, xs=xs: e.tensor_copy(out=XL[:, 1 - xs, 0:4], in_=XL[:, xs, 512:516]),
                       [f"XL{xs}"], [f"XLh{1 - xs}"])
                    GP(lambda e, LWs=LWs, xs=xs, b=b: e.tensor_scalar(out=LWs[0], in0=XL[:, xs, 4:516], scalar1=SMALL[:, 8 + b * 4 + 3:8 + b * 4 + 4],
                                                               scalar2=SMALL[:, 24 + b:25 + b], op0=ALU.mult, op1=ALU.add),
                       [f"XL{xs}", "SMALL"], [nm("XC")])
                    for k in range(3):
                        V(lambda e, xs=xs, b=b, k=k, LWs=LWs: e.scalar_tensor_tensor(
                            out=LWs[0], in0=XL[:, xs, 1 + k:513 + k], scalar=SMALL[:, 8 + b * 4 + k:8 + b * 4 + k + 1],
                            in1=LWs[0], op0=ALU.mult, op1=ALU.add),
                            [f"XL{xs}", f"XLh{xs}", "SMALL", nm("XC")], [nm("XC")])
                    ctx.append(dict(tt=tt, ts=ts, LWs=LWs, XCBs=XCBs, nm=nm, pgl=pgl, hs=tt % 2))
                for c in ctx:
                    A(lambda e, c=c: e.activation(out=c["XCBs"], in_=c["LWs"][0], func=AF.Copy), [c["nm"]("XC")], [c["nm"]("XCB")])
                for c in ctx:
                    pr = self.rot("psE", 8)
                    T(lambda e, c=c, pr=pr, b=b: e.matmul(self.ps[pr][:, :], lhsT=WG[:, 0, b, :], rhs=c["XCBs"], start=True, stop=True),
                      ["WG", c["nm"]("XCB")], [f"ps{pr}"])
                    pi = self.rot("psE", 8)
                    T(lambda e, c=c, pi=pi, b=b: e.matmul(self.ps[pi][:, :], lhsT=WG[:, 1, b, :], rhs=c["XCBs"], start=True, stop=True),
                      ["WG", c["nm"]("XCB")], [f"ps{pi}"])
                    c["pr"], c["pi"] = pr, pi
                for c in ctx:
                    A(lambda e, c=c, b=b: e.activation(out=c["LWs"][1], in_=self.ps[c["pr"]][:, :], func=AF.Sigmoid,
                                                      bias=SMALL[:, 28 + b:29 + b], scale=1.0), [f"ps{c['pr']}", "SMALL"], [c["nm"]("RR")])
                    A(lambda e, c=c, b=b: e.activation(out=c["LWs"][2], in_=self.ps[c["pi"]][:, :], func=AF.Sigmoid,
                                                      bias=SMALL[:, 32 + b:33 + b], scale=1.0), [f"ps{c['pi']}", "SMALL"], [c["nm"]("II")])
                for c in ctx:
                    A(lambda e, c=c, b=b: e.activation(out=c["LWs"][3], in_=c["LWs"][1], func=AF.Exp, scale=C12[:, 4 + b:5 + b]),
                      [c["nm"]("RR"), "C12b"], [c["nm"]("AA")])
                    A(lambda e, c=c, b=b: e.activation(out=c["LWs"][4], in_=c["LWs"][1], func=AF.Exp, scale=C12[:, 8 + b:9 + b]),
                      [c["nm"]("RR"), "C12c"], [c["nm"]("A2")])
                for c in ctx:
                    A(lambda e, c=c: e.activation(out=c["LWs"][4], in_=c["LWs"][4], func=AF.Sqrt, bias=1.0, scale=-1.0), [c["nm"]("A2")], [c["nm"]("A2")])
                for c in ctx:
                    nm_ = c["nm"]
                    V(lambda e, c=c: e.tensor_tensor(out=c["LWs"][5], in0=c["LWs"][4], in1=c["LWs"][2], op=ALU.mult), [nm_("A2"), nm_("II")], [nm_("BX")])
                    V(lambda e, c=c: e.tensor_tensor(out=c["LWs"][5], in0=c["LWs"][5], in1=c["LWs"][0], op=ALU.mult), [nm_("BX"), nm_("XC")], [nm_("BX")])
                    hs = c["hs"]
                    if c["tt"] == 0:
                        V(lambda e, c=c, hs=hs: e.tensor_tensor_scan(out=HL[:, hs, :], data0=c["LWs"][3], data1=c["LWs"][5], initial=0.0,
                                                                   op0=ALU.mult, op1=ALU.add), [nm_("AA"), nm_("BX")], [f"HL{hs}"])
                    else:
                        V(lambda e, c=c, hs=hs: e.tensor_tensor_scan(out=HL[:, hs, :], data0=c["LWs"][3], data1=c["LWs"][5],
                                                                   initial=HL[:, 1 - hs, 511:512], op0=ALU.mult, op1=ALU.add),
                          [nm_("AA"), nm_("BX"), f"HL{1 - hs}"], [f"HL{hs}"])
                for c in ctx:
                    A(lambda e, c=c: e.activation(out=c["LWs"][6], in_=self.ps[c["pgl"]][:, :], func=AF.Gelu_apprx_tanh),
                      [f"ps{c['pgl']}"], [c["nm"]("GG")])
                for c in ctx:
                    V(lambda e, c=c, b=b: e.tensor_tensor(out=MERGED[:, 4 + b, c["ts"]], in0=HL[:, c["hs"], :], in1=c["LWs"][6], op=ALU.mult),
                      [f"HL{c['hs']}", c["nm"]("GG")], [f"MG{4 + b}_{c['tt']}"])
        if self.lru_pending is not None:
            self.lru_pending()
            self.lru_pending = None
        P.barrier()

        gam = [1.0 - 2.0 ** (-5.0 - h) for h in range(4)]
        NT = 3

        def wm_load(h):
            return [lambda: P.op("gpsimd", lambda e: e.dma_start(out=WM[:, :, :], in_=self.d_wh[h].rearrange("(k p) n -> p k n", p=128)),
                                 w=["WM"], dma="WM")]

        def p_units(h):
            sl = h % 2
            units = []
            for (c0, dst, dn) in ((0, QP2[:, sl], "QP"), (256, KP2[:, sl], "KP")):
                for tt in range(NTT):
                    def u(c0=c0, dst=dst, dn=dn, tt=tt):
                        ts = slice(tt * TT, (tt + 1) * TT)
                        p0 = self.rot("psR", 6)
                        proj_fm(c0, tt, p0)
                        p1 = self.rot("psR", 6)
                        proj_fm(c0 + 128, tt, p1)
                        ta = self.rot("tmp", NT)
                        tb = self.rot("tmp", NT)
                        V(lambda e: e.tensor_tensor(out=TMP[:, ta, :], in0=self.ps[p0][:, :], in1=COS[:, ts], op=ALU.mult),
                          [f"ps{p0}", "COS"], [f"TMP{ta}"])
                        V(lambda e: e.tensor_tensor(out=TMP[:, tb, :], in0=self.ps[p1][:, :], in1=SIN[:, ts], op=ALU.mult),
                          [f"ps{p1}", "SIN"], [f"TMP{tb}"])
                        GP(lambda e: e.tensor_tensor(out=dst[:, ts], in0=TMP[:, ta, :], in1=TMP[:, tb, :], op=ALU.add),
                           [f"TMP{ta}", f"TMP{tb}"], [f"{dn}{sl}_{tt}"])
                    units.append(u)
            return units

        def vg_units(h):
            sl = h % 2
            VT, SGR = VT2[:, sl], SGR2[:, sl]
            units = []
            for blk in range(16):
                def u(blk=blk):
                    tt = blk // 4
                    pv = self.rot("psR", 6)

                    def fv(e):
                        for ck in range(NCK):
                            ins = e.matmul(self.ps[pv][:, 0:256], lhsT=self.XN[:, ck, blk * 128:(blk + 1) * 128], rhs=WM[:, ck, 512:768],
                                           start=(ck == 0), stop=(ck == NCK - 1))
                        return ins
                    T(fv, ["WM"] + [f"XN{ck}_{tt}" for ck in range(NCK)], [f"ps{pv}"])
                    A(lambda e: e.activation(out=VT[:, blk, :], in_=self.ps[pv][:, 0:128], func=AF.Copy), [f"ps{pv}"], [f"VT{sl}_{blk}"])
                    A(lambda e: e.activation(out=SGR[:, blk, :], in_=self.ps[pv][:, 128:256], func=AF.Silu), [f"ps{pv}"], [f"SGR{sl}_{blk}"])
                units.append(u)
            return units

        def ktr_units(h):
            sl = h % 2
            KP = KP2[:, sl]
            units = []
            for ng in range(4):
                def u(ng=ng):
                    pk = self.rot("psR", 6)

                    def fk(e):
                        for q in range(4):
                            n = ng * 4 + q
                            ins = e.transpose(out=psb[pk][:, q * 128:(q + 1) * 128], in_=KP[:, n * 128:(n + 1) * 128], identity=IDENT[:, :])
                        return ins
                    T(fk, ["IDENT", f"KP{sl}_{ng}"], [f"ps{pk}"])
                    A(lambda e: e.activation(out=KT[:, ng * 4:(ng + 1) * 4, :].rearrange("p a b -> p (a b)"),
                                             in_=psb[pk][:, 0:512], func=AF.Copy, scale=SMALL[:, 4 + h:5 + h]),
                      [f"ps{pk}", "SMALL"], [f"KT{ng}"])
                units.append(u)
            return units

        def rr_units(h):
            sl = h % 2
            VT = VT2[:, sl]
            c128 = gam[h] ** 128
            units = []

            def init():
                V(lambda e: e.memset(R[:, :], 0.0), [], ["R"])
                V(lambda e: e.memset(RB[:, 0, :], 0.0), [], ["RB0"])
            units.append(init)
            for n in range(15):
                def u(n=n):
                    pkv = self.rot("psR", 6)
                    T(lambda e: e.matmul(self.ps[pkv][:, 0:128], lhsT=KT[:, n, :], rhs=VT[:, n, :], start=True, stop=True),
                      [f"KT{n // 4}", f"VT{sl}_{n}"], [f"ps{pkv}"])
                    V(lambda e: e.scalar_tensor_tensor(out=R[:, :], in0=R[:, :], scalar=c128, in1=self.ps[pkv][:, 0:128],
                                                      op0=ALU.mult, op1=ALU.add), ["R", f"ps{pkv}"], ["R"])
                    A(lambda e: e.activation(out=RB[:, n + 1, :], in_=R[:, :], func=AF.Copy), ["R"], [f"RB{n + 1}"])
                units.append(u)
            return units

        def cl_units(h):
            sl = h % 2
            QP, KP, VT, SGR = QP2[:, sl], KP2[:, sl], VT2[:, sl], SGR2[:, sl]
            v3 = lambda ap: ap.rearrange("p (a b) -> p a b", b=128)
            bc = lambda ap: ap.unsqueeze(2).to_broadcast([128, 4, 128])
            units = []
            for ng in range(4):
                st8 = {}

                def start(ng=ng, st8=st8):
                    st8["po"] = 6 + self.rot("psO", 2)
                units.append(start)
                for q in range(4):
                    def u(ng=ng, q=q, st8=st8):
                        po = st8["po"]
                        n = ng * 4 + q
                        cs = slice(n * 128, (n + 1) * 128)
                        psc = self.rot("psR", 6)
                        T(lambda e: e.matmul(self.ps[psc][:, 0:128], lhsT=KP[:, cs], rhs=QP[:, cs], start=True, stop=True),
                          [f"KP{sl}_{ng}", f"QP{sl}_{ng}"], [f"ps{psc}"])
                        a_s = self.rot("as", 2)
                        V(lambda e: e.tensor_tensor(out=AS[:, a_s, :], in0=self.ps[psc][:, 0:128], in1=MASK[:, h, :], op=ALU.mult),
                          [f"ps{psc}", "MASK"], [f"AS{a_s}"])

                        def fo(e):
                            e.matmul(self.ps[po][:, q * 128:(q + 1) * 128], lhsT=AS[:, a_s, :], rhs=VT[:, n, :], start=True, stop=False)
                            return e.matmul(self.ps[po][:, q * 128:(q + 1) * 128], lhsT=QP[:, cs], rhs=RB[:, n, :], start=False, stop=True)
                        T(fo, [f"AS{a_s}", f"VT{sl}_{n}", f"QP{sl}_{ng}", f"RB{n}"], [f"ps{po}"])
                    units.append(u)

                def ln(ng=ng, st8=st8):
                    po = st8["po"]
                    t0 = self.rot("tmp", NT)
                    t1 = self.rot("tmp", NT)
                    t2 = self.rot("tmp", NT)
                    slt = self.rot("st", 2) * 4
                    A(lambda e: e.activation(out=TMP[:, t0, :], in_=self.ps[po][:, :], func=AF.Copy, scale=SMALL[:, h:h + 1]),
                      [f"ps{po}", "SMALL"], [f"TMP{t0}"])
                    V(lambda e: e.tensor_reduce(out=ST[:, slt, :], in_=v3(TMP[:, t0, :]), axis=AX.X, op=ALU.add), [f"TMP{t0}"], [f"ST{slt}"])
                    V(lambda e: e.tensor_scalar(out=ST[:, slt, :], in0=ST[:, slt, :], scalar1=-1.0 / 128, scalar2=None, op0=ALU.mult),
                      [f"ST{slt}"], [f"ST{slt}"])
                    V(lambda e: e.tensor_tensor(out=v3(TMP[:, t1, :]), in0=v3(TMP[:, t0, :]), in1=bc(ST[:, slt, :]), op=ALU.add),
                      [f"TMP{t0}", f"ST{slt}"], [f"TMP{t1}"])
                    GP(lambda e: e.tensor_tensor(out=TMP[:, t2, :], in0=TMP[:, t1, :], in1=TMP[:, t1, :], op=ALU.mult), [f"TMP{t1}"], [f"TMP{t2}"])
                    V(lambda e: e.tensor_reduce(out=ST[:, slt + 1, :], in_=v3(TMP[:, t2, :]), axis=AX.X, op=ALU.add), [f"TMP{t2}"], [f"ST{slt + 1}"])
                    A(lambda e: e.activation(out=ST[:, slt + 1, :], in_=ST[:, slt + 1, :], func=AF.Sqrt, bias=EPS, scale=1.0 / 128),
                      [f"ST{slt + 1}"], [f"ST{slt + 1}"])
                    V(lambda e: e.reciprocal(out=ST[:, slt + 1, :], in_=ST[:, slt + 1, :]), [f"ST{slt + 1}"], [f"ST{slt + 1}"])
                    V(lambda e: e.tensor_tensor(out=v3(TMP[:, t1, :]), in0=v3(TMP[:, t1, :]), in1=bc(ST[:, slt + 1, :]), op=ALU.mult),
                      [f"TMP{t1}", f"ST{slt + 1}"], [f"TMP{t1}"])
                    GP(lambda e: e.tensor_tensor(out=v3(TMP[:, t1, :]), in0=v3(TMP[:, t1, :]),
                                                 in1=GRET[:, h, :].unsqueeze(1).to_broadcast([128, 4, 128]), op=ALU.mult),
                       [f"TMP{t1}", "GRET"], [f"TMP{t1}"])
                    V(lambda e: e.tensor_tensor(out=MT[:, :, :], in0=v3(TMP[:, t1, :]), in1=SGR[:, ng * 4:(ng + 1) * 4, :], op=ALU.mult),
                      [f"TMP{t1}"] + [f"SGR{sl}_{ng * 4 + q}" for q in range(4)], ["MT"])
                    pt = self.rot("psR", 6)

                    def ft(e):
                        for q in range(4):
                            ins = e.transpose(out=psb[pt][:, q * 128:(q + 1) * 128], in_=MT[:, q, :], identity=IDENT[:, :])
                        return ins
                    T(ft, ["MT", "IDENT"], [f"ps{pt}"])
                    A(lambda e: e.activation(out=MERGED[:, h, ng * TT:(ng + 1) * TT], in_=psb[pt][:, 0:512], func=AF.Copy),
                      [f"ps{pt}"], [f"MG{h}_{ng}"])
                units.append(ln)
            return units

        for u in wm_load(0) + p_units(0) + vg_units(0) + ktr_units(0):
            u()
        for h in range(4):
            la = rr_units(h) + cl_units(h)
            lb = (wm_load(h + 1) + p_units(h + 1) + vg_units(h + 1)) if h + 1 < 4 else []
            for i in range(max(len(la), len(lb))):
                if i < len(la):
                    la[i]()
                if i < len(lb):
                    lb[i]()
            if h + 1 < 4:
                for u in ktr_units(h + 1):
                    u()
        P.barrier()
        for half in range(2):
            P.op("gpsimd", lambda e, half=half: e.dma_start(
                out=WM[:, :, 0:512], in_=self.d_wout.rearrange("(k p) n -> p k n", p=128)[:, :, half * 512:(half + 1) * 512]),
                w=["WM"], dma="WM")
            for tt in range(NTT):
                ts = slice(tt * TT, (tt + 1) * TT)
                for c4 in range(4):
                    c = half * 4 + c4
                    pw = self.rot("psE", 8)

                    def fw(e, c4=c4, ts=ts, pw=pw):
                        for k in range(NCK):
                            ins = e.matmul(self.ps[pw][:, :], lhsT=WM[:, k, c4 * 128:(c4 + 1) * 128], rhs=MERGED[:, k, ts],
                                           start=(k == 0), stop=(k == NCK - 1))
                        return ins
                    T(fw, ["WM"] + [f"MG{k}_{tt}" for k in range(NCK)], [f"ps{pw}"])
                    V(lambda e, c=c, ts=ts, pw=pw: e.tensor_tensor(out=self.X[:, c, ts], in0=self.ps[pw][:, :], in1=self.X[:, c, ts], op=ALU.add),
                      [f"ps{pw}", f"X{c}_{tt}"], [f"X{c}_{tt}"])
                if half == 1:
                    self.tail_norm(tt)
        P.barrier()


    def s5(self):
        P = self.P
        V = lambda fn, r, w: P.op("vector", fn, r=r, w=w)
        A = lambda fn, r, w: P.op("scalar", fn, r=r, w=w)
        T = lambda fn, r, w: P.op("tensor", fn, r=r, w=w)
        GP = lambda fn, r, w: P.op("gpsimd", fn, r=r, w=w)
        PI = math.pi
        TWO_PI = 2.0 * math.pi
        P.barrier()
        o = 3072
        RSALL = self.carve(o, [S], F32); o += S
        IOTA = self.carve(o, [260], F32); o += 260
        PM = self.carve(o, [4], F32); o += 4
        PP = self.carve(o, [3, 32], F32); o += 96
        BC = self.carve(o, [4, 32, 16], F32); o += 2048
        SC = self.carve(o, [12, 32], F32); o += 384
        PW = self.carve(o, [6, 32, 9], F32); o += 6 * 288
        BB = self.carve(o, [3, 32, 16], F32); o += 1536
        GD = self.carve(o, [8], F32); o += 8
        D8 = self.carve(o, [8], F32); o += 8
        ET = self.carve(o, [3, 4, 8, 16], F32); o += 1536
        E2 = self.carve(o, [8, 2, 128], BF16); o += 1024
        CL = self.carve(o, [3, 4, 9, 16], F32); o += 1728
        CT9_2 = self.carve(o, [2, 4, 9, 64], BF16); o += 2304
        BBS2 = self.carve(o, [2, 2, 128], BF16); o += 256
        ANG = self.carve(o, [5, 260], F32); o += 1300
        RF = self.carve(o, [256], F32); o += 256
        TM = self.carve(o, [4, 256], F32); o += 1024
        EE = self.carve(o, [2, 256], F32); o += 512
        STT = self.carve(o, [2, 260], F32); o += 520
        SB2 = self.carve(o, [2, 4, 2, 256], BF16); o += 2048
        SK = self.carve(o, [S], F32); o += S
        IDENT = self.carve(o, [128], BF16); o += 64
        assert o <= self.AW, o
        WGL = self.carve(5120, [2, NCK, 512], BF16)
        psb = [p[:, :].bitcast(BF16) for p in self.ps]

        def L(dst, src, key):
            P.op("sync", lambda e: e.dma_start(out=dst, in_=src), w=[key], dma="s5" + key)
        L(IOTA, self.d_iota, "IOTA")
        L(PM, self.d_pm, "PM")
        L(PP, self.d_s5pp, "PP")
        L(BC, self.d_s5bc, "BC")
        L(D8, self.d_s5d, "D8")
        P.op("gpsimd", lambda e: e.dma_start(out=IDENT, in_=self.d_ident), w=["IDENT5"], dma="IDENT5")
        lre, lim, ldt = PP[:, 0, :], PP[:, 1, :], PP[:, 2, :]
        DT, AA, TH, PHI, R8, NR, DEN, FR, FI, T0, T1 = [SC[:, k, :] for k in range(11)]
        GP(lambda e: e.tensor_tensor(out=GD[:, :], in0=self.gains[:, 40:48], in1=D8[:, :], op=ALU.mult), ["gains", "D8"], ["GD"])
        A(lambda e: e.activation(out=DT, in_=ldt, func=AF.Exp), ["PP"], ["DT"])
        GP(lambda e: e.tensor_tensor(out=AA, in0=lre, in1=DT, op=ALU.mult), ["PP", "DT"], ["AA"])
        GP(lambda e: e.tensor_tensor(out=TH, in0=lim, in1=DT, op=ALU.mult), ["PP", "DT"], ["TH"])
        MAGIC = 12582912.0
        INV2PI = 1.0 / TWO_PI
        SS = 0.99995

        def reduce_angle(dst, src, tmp, rd, wr):
            V(lambda e: e.tensor_scalar(out=tmp, in0=src, scalar1=INV2PI, scalar2=MAGIC, op0=ALU.mult, op1=ALU.add), rd, wr + ["_rt"])
            V(lambda e: e.tensor_scalar(out=tmp, in0=tmp, scalar1=-MAGIC, scalar2=-TWO_PI, op0=ALU.add, op1=ALU.mult), rd + ["_rt"], wr + ["_rt"])
            V(lambda e: e.tensor_tensor(out=dst, in0=src, in1=tmp, op=ALU.add), rd + ["_rt"], wr + ["_rt"])
        def red_pool0(dst, src, tmp, rd, wr):
            GP(lambda e: e.tensor_scalar(out=tmp, in0=src, scalar1=INV2PI, scalar2=MAGIC, op0=ALU.mult, op1=ALU.add), rd, wr + ["_rtq"])
            GP(lambda e: e.tensor_scalar(out=tmp, in0=tmp, scalar1=-MAGIC, scalar2=-TWO_PI, op0=ALU.add, op1=ALU.mult), rd + ["_rtq"], wr + ["_rtq"])
            GP(lambda e: e.tensor_tensor(out=dst, in0=src, in1=tmp, op=ALU.add), rd + ["_rtq"], wr + ["_rtq"])
        GP(lambda e: e.tensor_scalar(out=PHI, in0=TH, scalar1=8.0, scalar2=None, op0=ALU.mult), ["TH"], ["PHI"])
        red_pool0(PHI, PHI, T0, ["PHI"], ["PHI", "T0"])
        A(lambda e: e.activation(out=R8, in_=AA, func=AF.Exp, scale=8.0), ["AA"], ["R8"])
        i9 = IOTA[:, 0:9].unsqueeze(1).to_broadcast([128, 32, 9])
        bc9 = lambda ap: ap.unsqueeze(2).to_broadcast([128, 32, 9])
        GP(lambda e: e.tensor_tensor(out=PW[:, 0, :, :], in0=bc9(TH), in1=i9, op=ALU.mult), ["TH", "IOTA"], ["PW0"])
        GP(lambda e: e.tensor_tensor(out=PW[:, 2, :, :], in0=bc9(AA), in1=i9, op=ALU.mult), ["AA", "IOTA"], ["PW2"])
        A(lambda e: e.activation(out=PW[:, 2, :, :], in_=PW[:, 2, :, :], func=AF.Exp), ["PW2"], ["PW2"])
        GP(lambda e: e.tensor_scalar(out=PW[:, 1, :, :], in0=PW[:, 0, :, :], scalar1=0.5 * PI, scalar2=None, op0=ALU.add), ["PW0"], ["PW1"])
        red_pool0(PW[:, 0, :, :], PW[:, 0, :, :], PW[:, 5, :, :], ["PW0"], ["PW0", "PW5"])
        red_pool0(PW[:, 1, :, :], PW[:, 1, :, :], PW[:, 5, :, :], ["PW1"], ["PW1", "PW5"])
        A(lambda e: e.activation(out=PW[:, 0, :, :], in_=PW[:, 0, :, :], func=AF.Sin, scale=SS), ["PW0"], ["PW0"])
        A(lambda e: e.activation(out=PW[:, 1, :, :], in_=PW[:, 1, :, :], func=AF.Sin, scale=SS), ["PW1"], ["PW1"])
        GP(lambda e: e.tensor_tensor(out=PW[:, 3, :, :], in0=PW[:, 2, :, :], in1=PW[:, 1, :, :], op=ALU.mult), ["PW2", "PW1"], ["PWR"])
        GP(lambda e: e.tensor_tensor(out=PW[:, 4, :, :], in0=PW[:, 2, :, :], in1=PW[:, 0, :, :], op=ALU.mult), ["PW2", "PW0"], ["PWI"])
        PWR, PWI = PW[:, 3, :, :], PW[:, 4, :, :]
        GP(lambda e: e.tensor_scalar(out=NR, in0=PWR[:, :, 1], scalar1=-1.0, scalar2=None, op0=ALU.add), ["PWR"], ["NR"])
        NI = PWI[:, :, 1]
        GP(lambda e: e.tensor_tensor(out=DEN, in0=lre, in1=lre, op=ALU.mult), ["PP"], ["DEN"])
        GP(lambda e: e.tensor_tensor(out=T0, in0=lim, in1=lim, op=ALU.mult), ["PP"], ["T0"])
        GP(lambda e: e.tensor_tensor(out=DEN, in0=DEN, in1=T0, op=ALU.add), ["DEN", "T0"], ["DEN"])
        V(lambda e: e.reciprocal(out=DEN, in_=DEN), ["DEN"], ["DEN"])
        GP(lambda e: e.tensor_tensor(out=FR, in0=NR, in1=lre, op=ALU.mult), ["NR", "PP"], ["FR"])
        GP(lambda e: e.tensor_tensor(out=T0, in0=NI, in1=lim, op=ALU.mult), ["PWI", "PP", "DEN"], ["T0"])
        GP(lambda e: e.tensor_tensor(out=FR, in0=FR, in1=T0, op=ALU.add), ["FR", "T0"], ["FR"])
        GP(lambda e: e.tensor_tensor(out=FR, in0=FR, in1=DEN, op=ALU.mult), ["FR", "DEN"], ["FR"])
        GP(lambda e: e.tensor_tensor(out=FI, in0=NI, in1=lre, op=ALU.mult), ["PWI", "PP"], ["FI"])
        GP(lambda e: e.tensor_tensor(out=T1, in0=NR, in1=lim, op=ALU.mult), ["NR", "PP"], ["T1"])
        GP(lambda e: e.tensor_tensor(out=FI, in0=FI, in1=T1, op=ALU.subtract), ["FI", "T1"], ["FI"])
        GP(lambda e: e.tensor_tensor(out=FI, in0=FI, in1=DEN, op=ALU.mult), ["FI", "DEN"], ["FI"])
        bc16 = lambda ap: ap.unsqueeze(2).to_broadcast([128, 32, 16])
        bre, bim, cre, cim = BC[:, 0, :, :], BC[:, 1, :, :], BC[:, 2, :, :], BC[:, 3, :, :]
        BBR, BBI, BBT = BB[:, 0, :, :], BB[:, 1, :, :], BB[:, 2, :, :]
        GP(lambda e: e.tensor_tensor(out=BBR, in0=bre, in1=bc16(FR), op=ALU.mult), ["BC", "FR"], ["BBR"])
        GP(lambda e: e.tensor_tensor(out=BBT, in0=bim, in1=bc16(FI), op=ALU.mult), ["BC", "FI"], ["BBT"])
        GP(lambda e: e.tensor_tensor(out=BBR, in0=BBR, in1=BBT, op=ALU.subtract), ["BBR", "BBT"], ["BBR"])
        GP(lambda e: e.tensor_tensor(out=BBI, in0=bim, in1=bc16(FR), op=ALU.mult), ["BC", "FR"], ["BBI"])
        GP(lambda e: e.tensor_tensor(out=BBT, in0=bre, in1=bc16(FI), op=ALU.mult), ["BC", "FI", "BBR"], ["BBT"])
        GP(lambda e: e.tensor_tensor(out=BBI, in0=BBI, in1=BBT, op=ALU.add), ["BBI", "BBT"], ["BBI"])
        V(lambda e: e.memset(STT[:, :, 0:1], 0.0), [], ["STTz"])
        self.rmsnorm(5, rs_all=RSALL, deint=True)
        P.barrier()
        self.dbg("PWR", PW[:, 3, :, :], [128, 32, 9], F32, ["PWR"])
        self.dbg("PWI", PW[:, 4, :, :], [128, 32, 9], F32, ["PWI"])
        self.dbg("BBR", BBR, [128, 32, 16], F32, ["BBR"])
        self.dbg("BBI", BBI, [128, 32, 16], F32, ["BBI"])
        self.dbg("PHI", PHI, [128, 32], F32, ["PHI"])
        self.dbg("R8", R8, [128, 32], F32, ["R8"])


        BT2 = self.carve(0, [2, 8, 2, 128], BF16)
        KF2 = self.carve(2048, [2, 8, 128], BF16)
        A1, SN, CS, ATMP, A2 = [ANG[:, k, 0:257] for k in range(5)]

        def red_pool(dst, src, tmp, rd, wr):
            GP(lambda e: e.tensor_scalar(out=tmp, in0=src, scalar1=INV2PI, scalar2=MAGIC, op0=ALU.mult, op1=ALU.add), rd, wr + ["_rtp"])
            GP(lambda e: e.tensor_scalar(out=tmp, in0=tmp, scalar1=-MAGIC, scalar2=-TWO_PI, op0=ALU.add, op1=ALU.mult), rd + ["_rtp"], wr + ["_rtp"])
            GP(lambda e: e.tensor_tensor(out=dst, in0=src, in1=tmp, op=ALU.add), rd + ["_rtp"], wr + ["_rtp"])

        def tabE(ck):
            sl = ck % 2
            gs = slice(4 * ck, 4 * ck + 4)
            BT, KF, CT9, BBS, SB = BT2[:, sl], KF2[:, sl], CT9_2[:, sl], BBS2[:, sl], SB2[:, sl]
            bcE = lambda ap: ap.unsqueeze(2).to_broadcast([128, 4, 8, 16])
            pwE = lambda ap: ap.unsqueeze(3).to_broadcast([128, 4, 8, 16])
            Er, Ei, Et = ET[:, 0], ET[:, 1], ET[:, 2]
            V(lambda e: e.tensor_tensor(out=Er, in0=bcE(BBR[:, gs, :]), in1=pwE(PWR[:, gs, 0:8]), op=ALU.mult), ["BBR", "PWR"], ["Er"])
            V(lambda e: e.tensor_tensor(out=Et, in0=bcE(BBI[:, gs, :]), in1=pwE(PWI[:, gs, 0:8]), op=ALU.mult), ["BBI", "PWI"], ["Et"])
            V(lambda e: e.tensor_tensor(out=Er, in0=Er, in1=Et, op=ALU.subtract), ["Er", "Et"], ["Er"])
            V(lambda e: e.tensor_tensor(out=Ei, in0=bcE(BBR[:, gs, :]), in1=pwE(PWI[:, gs, 0:8]), op=ALU.mult), ["BBR", "PWI"], ["Ei"])
            V(lambda e: e.tensor_tensor(out=Et, in0=bcE(BBI[:, gs, :]), in1=pwE(PWR[:, gs, 0:8]), op=ALU.mult), ["BBI", "PWR", "Er"], ["Et"])
            V(lambda e: e.tensor_tensor(out=Ei, in0=Ei, in1=Et, op=ALU.add), ["Ei", "Et"], ["Ei"])
            E2v = E2.rearrange("p j r (g q c) -> p j r g q c", g=4, q=2)
            for ri, src, sn in ((0, Er, "Er"), (1, Ei, "Ei")):
                for q in range(2):
                    V(lambda e, ri=ri, src=src, q=q: e.tensor_scalar(
                        out=E2v[:, :, ri, :, q, :], in0=src.rearrange("p g j c -> p j g c"), scalar1=PM[:, q:q + 1], scalar2=None, op0=ALU.mult),
                        [sn, "PM"], [f"E2_{ri}"])
            for jq in range(4):
                pb = self.rot("psS", 2) + 6

                def ftr(e, jq=jq, pb=pb):
                    for u in range(4):
                        jp, ri = jq * 2 + u // 2, u % 2
                        ins = e.transpose(out=psb[pb][:, u * 128:(u + 1) * 128], in_=E2[:, jp, ri, :], identity=IDENT[:, :])
                    return ins
                T(ftr, ["E2_0", "E2_1", "IDENT5"], [f"ps{pb}"])
                A(lambda e, jq=jq, pb=pb: e.activation(out=BT[:, jq * 2:jq * 2 + 2, :, :].rearrange("p a b c -> p (a b c)"),
                                                       in_=psb[pb][:, 0:512], func=AF.Copy), [f"ps{pb}"], [f"BT{sl}_{jq}"])
            bcC = lambda ap: ap.unsqueeze(2).to_broadcast([128, 4, 9, 16])
            pwC = lambda ap: ap.unsqueeze(3).to_broadcast([128, 4, 9, 16])
            Cr, Ci, Ct = CL[:, 0], CL[:, 1], CL[:, 2]
            GP(lambda e: e.tensor_tensor(out=Cr, in0=bcC(cre[:, gs, :]), in1=pwC(PWR[:, gs, :]), op=ALU.mult), ["BC", "PWR"], ["Cr"])
            GP(lambda e: e.tensor_tensor(out=Ct, in0=bcC(cim[:, gs, :]), in1=pwC(PWI[:, gs, :]), op=ALU.mult), ["BC", "PWI"], ["Ct"])
            GP(lambda e: e.tensor_tensor(out=Cr, in0=Cr, in1=Ct, op=ALU.subtract), ["Cr", "Ct"], ["Cr"])
            GP(lambda e: e.tensor_tensor(out=Ci, in0=bcC(cre[:, gs, :]), in1=pwC(PWI[:, gs, :]), op=ALU.mult), ["BC", "PWI"], ["Ci"])
            GP(lambda e: e.tensor_tensor(out=Ct, in0=bcC(cim[:, gs, :]), in1=pwC(PWR[:, gs, :]), op=ALU.mult), ["BC", "PWR", "Cr"], ["Ct"])
            GP(lambda e: e.tensor_tensor(out=Ci, in0=Ci, in1=Ct, op=ALU.add), ["Ci", "Ct"], ["Ci"])
            CTv = CT9.rearrange("p g k (r q c) -> p g k r q c", r=2, q=2)
            for ri, src, sn in ((0, Cr, "Cr"), (1, Ci, "Ci")):
                for q in range(2):
                    A(lambda e, ri=ri, src=src, q=q: e.activation(
                        out=CTv[:, :, :, ri, q, :], in_=src, func=AF.Copy, scale=PM[:, 2 * ri + q:2 * ri + q + 1]),
                        [sn, "PM"], [f"CT9_{sl}"])
            BBv = BBS.rearrange("p r (g q c) -> p r g q c", g=4, q=2)
            for ri, src, sn in ((0, BBR, "BBR"), (1, BBI, "BBI")):
                for q in range(2):
                    A(lambda e, ri=ri, src=src, q=q: e.activation(
                        out=BBv[:, ri, :, q, :], in_=src[:, gs, :], func=AF.Copy, scale=PM[:, q:q + 1]),
                        [sn, "PM"], [f"BBS{sl}"])
            for hf in range(2):
                pk = 4 + hf
                V(lambda e, pk=pk: e.memset(self.ps[pk][:, :], 0.0), [], [f"ps{pk}"])

                def fkf(e, hf=hf, pk=pk):
                    for g4 in range(4):
                        for t4 in range(4):
                            tau = hf * 4 + t4
                            outv = self.ps[pk][32 * g4:32 * g4 + 32, t4 * 128 + 32 * g4:t4 * 128 + 32 * g4 + 32]
                            e.matmul(outv, lhsT=BBS[:, 0, 32 * g4:32 * g4 + 32], rhs=CT9[:, g4, tau, 0:32],
                                     start=True, stop=False, tile_position=(0, 32 * g4))
                            ins = e.matmul(outv, lhsT=BBS[:, 1, 32 * g4:32 * g4 + 32], rhs=CT9[:, g4, tau, 32:64],
                                           start=False, stop=True, tile_position=(0, 32 * g4))
                    return ins
                T(fkf, [f"BBS{sl}", f"CT9_{sl}"], [f"ps{pk}"])
                A(lambda e, hf=hf, pk=pk: e.activation(out=KF[:, hf * 4:hf * 4 + 4, :].rearrange("p a b -> p (a b)"), in_=self.ps[pk][:, :], func=AF.Copy),
                  [f"ps{pk}"], [f"KF{sl}_{hf}"])
            xn_all = [f"XN{ck}_{tt}" for tt in range(NTT)]
            uv = self.XN[:, ck, :].rearrange("p (j m) -> p j m", j=8)
            for g4 in range(4):
                gp = 4 * ck + g4
                pe = self.rot("psS", 2) + 6

                def fe(e, g4=g4, pe=pe):
                    for ri in range(2):
                        for j in range(8):
                            ins = e.matmul(self.ps[pe][:, ri * 256:(ri + 1) * 256], lhsT=BT[32 * g4:32 * g4 + 32, 7 - j, ri, :],
                                           rhs=uv[32 * g4:32 * g4 + 32, j, :], start=(j == 0), stop=(j == 7),
                                           tile_position=(32 * g4, 0))
                    return ins
                T(fe, [f"BT{sl}_{q}" for q in range(4)] + xn_all, [f"ps{pe}"])
                V(lambda e, gp=gp: e.tensor_scalar(out=A1, in0=IOTA[:, 0:257], scalar1=PHI[:, gp:gp + 1], scalar2=None, op0=ALU.mult),
                  ["IOTA", "PHI"], ["A1"])
                red_pool(SN, A1, ATMP, ["A1"], ["SN", "ATMP"])
                V(lambda e: e.tensor_scalar(out=A2, in0=A1, scalar1=0.5 * PI, scalar2=None, op0=ALU.add), ["A1"], ["A2"])
                reduce_angle(CS, A2, CS, ["A2"], ["CS"])
                A(lambda e: e.activation(out=SN, in_=SN, func=AF.Sin, scale=SS), ["SN"], ["SN"])
                A(lambda e: e.activation(out=CS, in_=CS, func=AF.Sin, scale=SS), ["CS"], ["CS"])
                GP(lambda e, gp=gp: e.tensor_scalar(out=RF[:, :], in0=IOTA[:, 0:256], scalar1=0.0, scalar2=R8[:, gp:gp + 1], op0=ALU.mult, op1=ALU.add),
                   ["IOTA", "R8"], ["RF"])
                ere, eim = self.ps[pe][:, 0:256], self.ps[pe][:, 256:512]
                V(lambda e, ere=ere: e.tensor_tensor(out=TM[:, 0, :], in0=ere, in1=CS[:, 1:257], op=ALU.mult), [f"ps{pe}", "CS"], ["TM0"])
                V(lambda e, eim=eim: e.tensor_tensor(out=TM[:, 1, :], in0=eim, in1=SN[:, 1:257], op=ALU.mult), [f"ps{pe}", "SN"], ["TM1"])
                V(lambda e, eim=eim: e.tensor_tensor(out=TM[:, 2, :], in0=eim, in1=CS[:, 1:257], op=ALU.mult), [f"ps{pe}", "CS"], ["TM2"])
                V(lambda e, ere=ere: e.tensor_tensor(out=TM[:, 3, :], in0=ere, in1=SN[:, 1:257], op=ALU.mult), [f"ps{pe}", "SN"], ["TM3"])
                GP(lambda e: e.tensor_tensor(out=EE[:, 0, :], in0=TM[:, 0, :], in1=TM[:, 1, :], op=ALU.add), ["TM0", "TM1"], ["EE0"])
                GP(lambda e: e.tensor_tensor(out=EE[:, 1, :], in0=TM[:, 2, :], in1=TM[:, 3, :], op=ALU.subtract), ["TM2", "TM3"], ["EE1"])
                for ri in range(2):
                    V(lambda e, ri=ri: e.tensor_tensor_scan(out=STT[:, ri, 1:257], data0=RF[:, :], data1=EE[:, ri, :], initial=0.0,
                                                           op0=ALU.mult, op1=ALU.add), ["RF", f"EE{ri}", "STTz"], [f"STT{ri}"])
                V(lambda e: e.tensor_tensor(out=TM[:, 0, :], in0=STT[:, 0, 0:256], in1=CS[:, 0:256], op=ALU.mult), ["STT0", "CS"], ["TM0"])
                V(lambda e: e.tensor_tensor(out=TM[:, 1, :], in0=STT[:, 1, 0:256], in1=SN[:, 0:256], op=ALU.mult), ["STT1", "SN"], ["TM1"])
                GP(lambda e: e.tensor_tensor(out=TM[:, 2, :], in0=STT[:, 0, 0:256], in1=SN[:, 0:256], op=ALU.mult), ["STT0", "SN"], ["TM2"])
                GP(lambda e: e.tensor_tensor(out=TM[:, 3, :], in0=STT[:, 1, 0:256], in1=CS[:, 0:256], op=ALU.mult), ["STT1", "CS"], ["TM3"])
                GP(lambda e, g4=g4: e.tensor_tensor(out=SB[:, g4, 0, :], in0=TM[:, 0, :], in1=TM[:, 1, :], op=ALU.subtract), ["TM0", "TM1"], [f"SB{sl}_{g4}"])
                GP(lambda e, g4=g4: e.tensor_tensor(out=SB[:, g4, 1, :], in0=TM[:, 2, :], in1=TM[:, 3, :], op=ALU.add), ["TM2", "TM3"], [f"SB{sl}_{g4}"])

        def ypart(ck):
            sl = ck % 2
            KF, CT9, SB = KF2[:, sl], CT9_2[:, sl], SB2[:, sl]
            xn_all = [f"XN{ck}_{tt}" for tt in range(NTT)]
            uv = self.XN[:, ck, :].rearrange("p (j m) -> p j m", j=8)
            for i in range(8):
                pb = i // 2
                reg = self.ps[pb][:, (i % 2) * 256:(i % 2 + 1) * 256]

                def fy(e, i=i, pb=pb, reg=reg):
                    for j in range(i + 1):
                        e.matmul(reg, lhsT=KF[:, i - j, :], rhs=uv[:, j, :], start=(j == 0), stop=False)
                    for g4 in range(4):
                        for ri in range(2):
                            ins = e.matmul(self.ps[pb][32 * g4:32 * g4 + 32, (i % 2) * 256:(i % 2 + 1) * 256],
                                           lhsT=CT9[:, g4, i + 1, ri * 32:(ri + 1) * 32], rhs=SB[:, g4, ri, :],
                                           start=False, stop=(g4 == 3 and ri == 1), tile_position=(0, 32 * g4))
                    return ins
                T(fy, [f"KF{sl}_0", f"KF{sl}_1", f"CT9_{sl}"] + [f"SB{sl}_{q}" for q in range(4)] + xn_all, [f"ps{pb}"])

        def evac(ck):
            for tt in range(NTT):
                ts = slice(tt * TT, (tt + 1) * TT)
                V(lambda e, ts=ts: e.scalar_tensor_tensor(out=SK[:, ts], in0=self.X[:, ck, ts], scalar=GD[:, ck:ck + 1], in1=RSALL[:, ts],
                                                         op0=ALU.mult, op1=ALU.mult), [f"X{ck}_{tt}", "GD", "RSALL"], [f"SK{tt}"])
            SKv = SK.rearrange("p (m i) -> p i m", i=8)
            for pb in range(4):
                V(lambda e, pb=pb: e.tensor_tensor(out=SKv[:, 2 * pb:2 * pb + 2, :], in0=self.ps[pb][:, :].rearrange("p (i m) -> p i m", i=2),
                                                   in1=SKv[:, 2 * pb:2 * pb + 2, :], op=ALU.add),
                  [f"ps{pb}"] + [f"SK{tt}" for tt in range(NTT)], [f"SK{tt}" for tt in range(NTT)])
            for tt in range(NTT):
                ts = slice(tt * TT, (tt + 1) * TT)
                A(lambda e, ts=ts: e.activation(out=self.XN[:, ck, ts], in_=SK[:, ts], func=AF.Gelu_apprx_tanh),
                  [f"SK{tt}"], [f"XN{ck}_{tt}"])

        tabE(0)
        for ck in range(NCK):
            ypart(ck)
            if ck + 1 < NCK:
                tabE(ck + 1)
            evac(ck)

        P.barrier()
        for half in range(2):
            for ab in range(2):
                src = self.d_glu[ab].rearrange("(k p) n -> p k n", p=128)[:, :, half * 512:(half + 1) * 512]
                P.op("gpsimd", lambda e, ab=ab, src=src: e.dma_start(out=WGL[:, ab, :, :], in_=src), w=[f"WGL{ab}"], dma=f"WGL{ab}")
            for tt in range(NTT):
                ts = slice(tt * TT, (tt + 1) * TT)
                for c4 in range(4):
                    c = half * 4 + c4
                    pa = self.rot("psG", 4)
                    pb_ = 4 + self.rot("psG2", 4)

                    def fg(e, ab, pbank, c4=c4, ts=ts):
                        for k in range(NCK):
                            ins = e.matmul(self.ps[pbank][:, :], lhsT=WGL[:, ab, k, c4 * 128:(c4 + 1) * 128], rhs=self.XN[:, k, ts],
                                           start=(k == 0), stop=(k == NCK - 1))
                        return ins
                    zr = [f"XN{k}_{tt}" for k in range(NCK)]
                    T(lambda e, fg=fg, pa=pa: fg(e, 0, pa), ["WGL0"] + zr, [f"ps{pa}"])
                    T(lambda e, fg=fg, pb_=pb_: fg(e, 1, pb_), ["WGL1"] + zr, [f"ps{pb_}"])
                    sg = self.rot("tm5", 2)
                    A(lambda e, sg=sg, pb_=pb_: e.activation(out=SK[:, sg * 512:(sg + 1) * 512], in_=self.ps[pb_][:, :], func=AF.Sigmoid),
                      [f"ps{pb_}"], [f"SK{sg}"])
                    V(lambda e, sg=sg, pa=pa: e.tensor_tensor(out=SK[:, sg * 512:(sg + 1) * 512], in0=self.ps[pa][:, :], in1=SK[:, sg * 512:(sg + 1) * 512], op=ALU.mult),
                      [f"ps{pa}", f"SK{sg}"], [f"SK{sg}"])
                    GP(lambda e, sg=sg, c=c, ts=ts: e.tensor_tensor(out=self.X[:, c, ts], in0=self.X[:, c, ts], in1=SK[:, sg * 512:(sg + 1) * 512], op=ALU.add),
                       [f"SK{sg}", f"X{c}_{tt}"], [f"X{c}_{tt}"])
                if half == 1:
                    self.tail_norm(tt)
        P.barrier()

    def final(self, sq):
        P = self.P
        if not self.norm_done:
            self.rmsnorm(6, out_bf16=False)
        self.norm_done = False
        for ck in range(NCK):
            P.op("sync",
                 lambda e, ck=ck: e.dma_start(out=self.outT[sq, ck * 128:(ck + 1) * 128, :], in_=self.X[:, ck, :]),
                 r=[f"X{ck}_{tt}" for tt in range(NTT)], dma="out")

    def store_x(self, sq):
        P = self.P
        for ck in range(NCK):
            P.op("sync",
                 lambda e, ck=ck: e.dma_start(out=self.outT[sq, ck * 128:(ck + 1) * 128, :], in_=self.X[:, ck, :]),
                 r=[f"X{ck}_{tt}" for tt in range(NTT)], dma="out")

    def build(self):
        st = self.stages
        self.setup()
        for sq in range(self.nseq):
            self.load_x(sq)
            for si, name in enumerate(st):
                nxt = st[si + 1] if si + 1 < len(st) else None
                self.next_norm = None
                if nxt is not None and HOIST_NORM:
                    if nxt.startswith("ffn"):
                        self.next_norm = dict(gidx=int(nxt[3]) * 2 + int(nxt[4]))
                    elif nxt == "even":
                        self.next_norm = dict(gidx=4)
                    elif nxt == "final":
                        self.next_norm = dict(gidx=6, out_bf16=False)
                if nxt == "even" and name.startswith("ffn"):
                    self.even_prefetch()
                if name == "final":
                    self.final(sq)
                elif name == "store":
                    self.store_x(sq)
                elif name == "even":
                    self.even()
                elif name == "s5":
                    self.s5()
                elif name.startswith("ffn"):
                    self.ffn(int(name[3]), int(name[4]))
                else:
                    raise ValueError(name)
        self.P.emit(["out"])
        self.es.close()
        return self.nc


ALL_STAGES = ["ffn00", "even", "ffn01", "ffn10", "s5", "ffn11", "final"]


def prep_inputs(inputs, b0, nseq):
    x = inputs["x"]
    m = {}
    m["xT"] = np.ascontiguousarray(np.transpose(x[b0:b0 + nseq], (0, 2, 1)))
    G0 = FFN_G
    ng = NFC // G0
    m["ffn_w1"] = np.ascontiguousarray(inputs["ffn_w1"].reshape(2, 2, NCK, 128, ng, G0 * 128).transpose(0, 1, 4, 3, 2, 5))
    m["ffn_w3"] = np.ascontiguousarray(inputs["ffn_w3"].reshape(2, 2, NCK, 128, ng, G0 * 128).transpose(0, 1, 4, 3, 2, 5))
    m["ffn_w2"] = np.ascontiguousarray(inputs["ffn_w2"].reshape(2, 2, ng, G0, 128, D).transpose(0, 1, 2, 4, 3, 5))
    g = np.concatenate([inputs["ffn_norm_g"].reshape(4, D), inputs["mix_norm_g"].reshape(2, D),
                        inputs["final_norm_g"].reshape(1, D)], axis=0)
    m["gains"] = np.ascontiguousarray(g.reshape(7, NCK, 128).transpose(2, 0, 1).reshape(128, 7 * NCK))
    m.update(even_consts())
    w_in = inputs["w_in_even"][0]
    perm = np.concatenate([np.arange(64, 128), np.arange(0, 64)])
    wh = np.empty((4, D, 768), np.float32)
    for h in range(4):
        q = w_in[:, h * 128:(h + 1) * 128]
        k = w_in[:, 512 + h * 128:512 + (h + 1) * 128]
        wh[h, :, 0:128] = q
        wh[h, :, 128:256] = q[:, perm]
        wh[h, :, 256:384] = k
        wh[h, :, 384:512] = k[:, perm]
        wh[h, :, 512:640] = w_in[:, 1024 + h * 128:1024 + (h + 1) * 128]
        wh[h, :, 640:768] = w_in[:, 1536 + h * 128:1536 + (h + 1) * 128]
    m["w_head"] = wh
    wl = np.empty((4, D, 256), np.float32)
    for b in range(4):
        wl[b, :, 0:128] = w_in[:, 2048 + b * 128:2048 + (b + 1) * 128]
        wl[b, :, 128:256] = w_in[:, 2560 + b * 128:2560 + (b + 1) * 128]
    m["w_lru"] = wl
    m["w_out_even"] = inputs["w_out_even"][0]
    m["gret"] = np.ascontiguousarray(np.broadcast_to(inputs["ret_norm_g"][0].reshape(1, 4, 128), (128, 4, 128)))
    small = m["even_small"]
    small[:, 8:24] = inputs["conv_w"][0].reshape(4, 4, 128).transpose(2, 1, 0).reshape(128, 16)
    small[:, 24:28] = inputs["conv_b"][0].reshape(4, 128).T
    small[:, 28:32] = inputs["lru_b_a"][0].reshape(4, 128).T
    small[:, 32:36] = inputs["lru_b_i"][0].reshape(4, 128).T
    small[:, 36:40] = inputs["lru_lambda"][0].reshape(4, 128).T
    def gp_layout(a):
        sh = a.shape[2:]
        return np.ascontiguousarray(a.reshape((32, 2, 64) + sh).transpose((1, 2, 0) + tuple(range(3, 3 + len(sh)))).reshape((128, 32) + sh))
    lre = gp_layout(inputs["s5_lambda_re"][0])
    lim = gp_layout(inputs["s5_lambda_im"][0])
    ldt = gp_layout(np.ascontiguousarray(np.broadcast_to(inputs["s5_log_dt"][0][:, None], (64, 64))))
    m["s5_pp"] = np.ascontiguousarray(np.stack([lre, lim, ldt], 1))
    bre = gp_layout(inputs["s5_b_re"][0])
    bim = gp_layout(inputs["s5_b_im"][0])
    cre = gp_layout(np.ascontiguousarray(inputs["s5_c_re"][0].transpose(0, 2, 1)))
    cim = gp_layout(np.ascontiguousarray(inputs["s5_c_im"][0].transpose(0, 2, 1)))
    m["s5_bc"] = np.ascontiguousarray(np.stack([bre, bim, cre, cim], 1))
    m["s5_d8"] = np.ascontiguousarray(inputs["s5_d"][0].reshape(8, 128).T)
    m["glu_w"] = np.ascontiguousarray(np.stack([inputs["glu_w_a"][0], inputs["glu_w_b"][0]], 0))
    m["lru_wg"] = np.ascontiguousarray(np.stack([inputs["lru_w_a"][0], inputs["lru_w_i"][0]], 0).transpose(2, 0, 1, 3))
    return m


def even_consts():
    half = 64
    inv = (np.float32(10000.0) ** (-np.arange(half, dtype=np.float32) / np.float32(half))).astype(np.float32)
    ang = (np.arange(S, dtype=np.float32)[:, None] * inv[None, :]).astype(np.float32)
    cos = np.cos(ang).astype(np.float32).T
    sin = np.sin(ang).astype(np.float32).T
    c = {}
    c["rope_cos"] = np.ascontiguousarray(np.concatenate([cos, cos], 0))
    c["rope_sin"] = np.ascontiguousarray(np.concatenate([-sin, sin], 0))
    gam = 1.0 - 2.0 ** (-5.0 - np.arange(4, dtype=np.float64))
    j = np.arange(128, dtype=np.float64)
    mask = np.zeros((128, 4, 128), np.float64)
    for h in range(4):
        mask[:, h, :] = (128.0 ** -0.5) * (gam[h] ** (-(j[:, None] + 1.0))) * (j[None, :] >= j[:, None])
    c["ret_mask"] = mask.astype(np.float32)
    small = np.zeros((128, 64), np.float32)
    for h in range(4):
        small[:, h] = gam[h] ** (j + 1.0)
        small[:, 4 + h] = (128.0 ** -0.5) * gam[h] ** (127.0 - j)
    c["even_small"] = small
    c["ident"] = np.eye(128, dtype=np.float32)
    c["iota"] = np.ascontiguousarray(np.broadcast_to(np.arange(260, dtype=np.float32)[None, :], (128, 260)))
    pm = np.zeros((128, 4), np.float32)
    pm[0:64, 0] = 1.0
    pm[64:128, 1] = 1.0
    pm[:, 2:4] = -pm[:, 0:2]
    c["parmask"] = pm
    return c


def kernel(**inputs):
    inputs = {k: np.asarray(v) for k, v in inputs.items()}
    ncores = 8
    nc = Builder(stages=ALL_STAGES).build()
    in_maps = [prep_inputs(inputs, c * NSEQ, NSEQ) for c in range(ncores)]
    res = run_bass_kernel_spmd(nc, in_maps, core_ids=list(range(ncores)))
    outs = [np.transpose(r["outT"], (0, 2, 1)) for r in res.results]
    return np.ascontiguousarray(np.concatenate(outs, axis=0)).astype(np.float32)
```

```python
import math
from contextlib import ExitStack
import numpy as np
import concourse.bass as bass
import concourse.mybir as mybir
from concourse.bass_utils import run_bass_kernel_spmd

F32 = mybir.dt.float32
BF16 = mybir.dt.bfloat16
ALU = mybir.AluOpType
AF = mybir.ActivationFunctionType
AX = mybir.AxisListType

D = 1024
S = 2048
NCK = 8
DFF = 2816
NFC = 22
TT = 512
NTT = S // TT
EPS = 1e-6
NSEQ = 2
EPOCH = 16000
FFN_G = 2
HOIST_NORM = False

ENGS = ["tensor", "vector", "scalar", "gpsimd", "sync"]


class Op:
    __slots__ = ("eng", "idx", "fn", "dma", "deps", "sig", "ticket", "dma_cnt")

    def __init__(self, eng, idx, fn, dma):
        self.eng = eng
        self.idx = idx
        self.fn = fn
        self.dma = dma
        self.deps = ()
        self.sig = False
        self.ticket = None
        self.dma_cnt = None


class Prog:
    def __init__(self, nc, es):
        self.nc = nc
        self.es = es
        self.ops = {e: [] for e in ENGS}
        self.lastw = {}
        self.readers = {}
        self.dma_count = {}
        self.same_engine_sync = True
        self.bar = set()
        self.last_dma = {}

    def barrier(self):
        b = set()
        for e in ENGS:
            if self.ops[e]:
                lo = self.ops[e][-1]
                if lo.dma is None:
                    b.add(lo)
                else:
                    for o in reversed(self.ops[e]):
                        if o.dma is None:
                            b.add(o)
                            break
        for k, o in self.last_dma.items():
            b.add(o)
        self.bar = b

    def op(self, eng, fn, r=(), w=(), dma=None):
        o = Op(eng, len(self.ops[eng]), fn, dma)
        deps = set()
        for res in r:
            lw = self.lastw.get(res)
            if lw is not None:
                deps.add(lw)
        for res in w:
            lw = self.lastw.get(res)
            if lw is not None:
                deps.add(lw)
            for rd in self.readers.get(res, ()):
                deps.add(rd)
        deps |= self.bar
        deps.discard(o)
        o.deps = deps
        if dma is not None:
            self.last_dma[dma] = o
        for res in w:
            self.lastw[res] = o
            self.readers[res] = []
        for res in r:
            if res not in w:
                self.readers.setdefault(res, []).append(o)
        if dma is not None:
            self.dma_count[dma] = self.dma_count.get(dma, 0) + 1
            o.dma_cnt = self.dma_count[dma]
        self.ops[eng].append(o)
        return o

    def emit(self, final_dma_keys):
        nc = self.nc
        for e in ENGS:
            for o in self.ops[e]:
                for d in o.deps:
                    if d.dma is None:
                        if d.eng == o.eng and (d.eng == "tensor" or not self.same_engine_sync):
                            continue
                        d.sig = True
        nep = {}
        for e in ENGS:
            c = 0
            for o in self.ops[e]:
                if o.sig:
                    o.ticket = (c // EPOCH, c % EPOCH + 1)
                    c += 1
            nep[e] = c // EPOCH + 1
        sems = {}
        for e in ENGS:
            for k in range(nep[e]):
                sems[(e, k)] = self.es.enter_context(nc.semaphore(f"s_{e}_{k}"))
        dsems = {}
        for key in self.dma_count:
            dsems[key] = self.es.enter_context(nc.semaphore(f"d_{key}"))
        block = self.es.enter_context(nc.Block())

        def run(engname, eng):
            waited = {}
            for o in self.ops[engname]:
                need = {}
                for d in o.deps:
                    if d.dma is not None:
                        k = ("dma", d.dma)
                        v = 16 * d.dma_cnt
                    else:
                        if d.eng == engname and (engname == "tensor" or not self.same_engine_sync):
                            continue
                        k = (d.eng, d.ticket[0])
                        v = d.ticket[1]
                    if v > need.get(k, 0):
                        need[k] = v
                for k, v in need.items():
                    if k[0] != "dma":
                        newer = [kk for kk in waited if kk[0] == k[0] and kk[0] != "dma" and kk[1] > k[1]]
                        if newer:
                            continue
                    if waited.get(k, 0) >= v:
                        continue
                    waited[k] = v
                    sem = dsems[k[1]] if k[0] == "dma" else sems[k]
                    eng.wait_ge(sem, v)
                ins = o.fn(eng)
                if o.dma is not None:
                    ins.then_inc(dsems[o.dma], 16)
                elif o.sig:
                    ins.then_inc(sems[(engname, o.ticket[0])], 1)
            if engname == "sync":
                for key in final_dma_keys:
                    eng.wait_ge(dsems[key], 16 * self.dma_count[key])

        @block.tensor
        def _(t):
            run("tensor", t)

        @block.vector
        def _(v):
            run("vector", v)

        @block.scalar
        def _(a):
            run("scalar", a)

        @block.gpsimd
        def _(g):
            run("gpsimd", g)

        @block.sync
        def _(s):
            run("sync", s)


class Builder:
    def __init__(self, stages=None, nseq=NSEQ, ffn_g=FFN_G):
        self.stages = stages
        self.nseq = nseq
        self.G = ffn_g
        self.es = ExitStack()
        nc = bass.Bass("TRN2", target_bir_lowering=False)
        self.nc = nc
        self.P = Prog(nc, self.es)
        es = self.es

        def din(name, shape):
            return nc.dram_tensor(name, list(shape), F32, kind="ExternalInput").ap()

        self.xT = din("xT", [nseq, D, S])
        self.outT = nc.dram_tensor("outT", [nseq, D, S], F32, kind="ExternalOutput").ap()
        G0 = ffn_g
        assert NFC % G0 == 0
        self.w1 = din("ffn_w1", [2, 2, NFC // G0, 128, NCK, G0 * 128])
        self.w3 = din("ffn_w3", [2, 2, NFC // G0, 128, NCK, G0 * 128])
        self.w2 = din("ffn_w2", [2, 2, NFC // G0, 128, G0, D])
        self.gains_d = din("gains", [128, 7 * NCK])
        self.d_cos = din("rope_cos", [128, S])
        self.d_sin = din("rope_sin", [128, S])
        self.d_mask = din("ret_mask", [128, 4, 128])
        self.d_gret = din("gret", [128, 4, 128])
        self.d_small = din("even_small", [128, 64])
        self.d_ident = din("ident", [128, 128])
        self.d_wg = din("lru_wg", [128, 2, 4, 128])
        self.d_wl = din("w_lru", [4, D, 256])
        self.d_wh = din("w_head", [4, D, 768])
        self.d_wout = din("w_out_even", [D, D])
        self.d_iota = din("iota", [128, 260])
        self.d_pm = din("parmask", [128, 4])
        self.d_s5pp = din("s5_pp", [128, 3, 32])
        self.d_s5bc = din("s5_bc", [128, 4, 32, 16])
        self.d_s5d = din("s5_d8", [128, 8])
        self.d_glu = din("glu_w", [2, D, D])

        def sb(name, shape, dt):
            return es.enter_context(nc.sbuf_tensor(name, list(shape), dt))

        self.X = sb("X", [128, NCK, S], F32)
        self.XN = sb("XN", [128, NCK, S], BF16)
        self.gains = sb("gains_sb", [128, 7 * NCK], F32)
        self.ones = sb("ones", [128, 128], BF16)
        self.negpi = sb("negpi", [128, 2], F32)
        AW = 28160
        self.ARENA = sb("ARENA", [128, AW], F32)
        self.AW = AW
        G = self.G
        self.SQ = self.carve(0, [NCK, TT], BF16)
        self.RS = self.carve(2048, [2, TT], F32)
        o = 3072
        self.W1S = self.carve(o, [2, NCK, G * 128], BF16); o += NCK * G * 128
        self.W3S = self.carve(o, [2, NCK, G * 128], BF16); o += NCK * G * 128
        self.W2S = self.carve(o, [2, G, D], BF16); o += G * D
        self.SG = self.carve(o, [2, TT], F32); o += 2 * TT
        self.GT = self.carve(o, [2, G, TT], BF16); o += G * TT
        assert o <= AW
        self.ps = [es.enter_context(nc.psum_tensor(f"ps{i}", [128, 512], F32)) for i in range(8)]
        self.rr = {}
        self.pending_down = None
        self.lru_pending = None
        self.next_norm = None
        self.norm_done = False
        self.even_prefetched = False

    def carve(self, off, shape, dt):
        n = 1
        for d in shape:
            n *= d
        words = n if dt == F32 else (n + 1) // 2
        assert off + words <= self.AW, (off, words)
        v = self.ARENA[:, off:off + words]
        if dt != F32:
            v = v.bitcast(dt)
        if len(shape) == 2:
            v = v.rearrange("p (a b) -> p a b", b=shape[1])
        elif len(shape) == 3:
            v = v.rearrange("p (a b c) -> p a b c", b=shape[1], c=shape[2])
        elif len(shape) == 4:
            v = v.rearrange("p (a b c d) -> p a b c d", b=shape[1], c=shape[2], d=shape[3])
        return v

    def dbg(self, name, ap, shape, dt, rd):
        if not getattr(self, "debug", False):
            return
        d = self.nc.dram_tensor("dbg_" + name, list(shape), dt, kind="ExternalOutput").ap()
        self.P.op("sync", lambda e: e.dma_start(out=d, in_=ap), r=rd, dma="out")

    def rot(self, key, n):
        v = self.rr.get(key, 0)
        self.rr[key] = v + 1
        return v % n

    def setup(self):
        P = self.P
        P.op("sync", lambda e: e.dma_start(out=self.gains[:], in_=self.gains_d), w=["gains"], dma="gains")
        P.op("vector", lambda e: e.memset(self.ones[:], 1.0 / D), w=["ones"])
        P.op("vector", lambda e: e.memset(self.negpi[:, 0:1], -math.pi), w=["negpi"])
        P.op("vector", lambda e: e.memset(self.negpi[:, 1:2], 0.5 * math.pi), w=["negpi"])

    def load_x(self, sq):
        P = self.P
        for ck in range(NCK):
            P.op("sync",
                 lambda e, ck=ck: e.dma_start(out=self.X[:, ck, :], in_=self.xT[sq, ck * 128:(ck + 1) * 128, :]),
                 w=[f"X{ck}_{tt}" for tt in range(NTT)], dma=f"x{ck}")

    def rmsnorm(self, gidx, out_bf16=True, rs_all=None, deint=False, tiles=None):
        P = self.P
        for tt in (range(NTT) if tiles is None else tiles):
            ts = slice(tt * TT, (tt + 1) * TT)
            for ck in range(NCK):
                P.op("scalar",
                     lambda e, ck=ck, ts=ts: e.activation(out=self.SQ[:, ck, :], in_=self.X[:, ck, ts], func=AF.Square),
                     r=[f"X{ck}_{tt}"], w=[f"SQ{ck}"])
            pb = 7

            def mm(e, pb=pb):
                for ck in range(NCK):
                    ins = e.matmul(self.ps[pb][:, :], lhsT=self.ones[:, :], rhs=self.SQ[:, ck, :],
                                   start=(ck == 0), stop=(ck == NCK - 1))
                return ins
            P.op("tensor", mm, r=["ones"] + [f"SQ{ck}" for ck in range(NCK)], w=[f"ps{pb}"])
            rs = self.rot("rs", 2)
            P.op("scalar",
                 lambda e, rs=rs, pb=pb: e.activation(out=self.RS[:, rs, :], in_=self.ps[pb][:, :], func=AF.Sqrt,
                                                      bias=EPS, scale=1.0),
                 r=[f"ps{pb}"], w=[f"RS{rs}"])
            P.op("vector", lambda e, rs=rs: e.reciprocal(out=self.RS[:, rs, :], in_=self.RS[:, rs, :]),
                 r=[f"RS{rs}"], w=[f"RS{rs}"])
            if rs_all is not None:
                P.op("gpsimd", lambda e, rs=rs, ts=ts: e.tensor_copy(out=rs_all[:, ts], in_=self.RS[:, rs, :]),
                     r=[f"RS{rs}"], w=["RSALL"])
            for ck in range(NCK):
                if out_bf16 and deint:
                    P.op("vector",
                         lambda e, ck=ck, ts=ts, rs=rs, tt=tt: e.scalar_tensor_tensor(
                             out=self.XN[:, ck, :].rearrange("p (j m) -> p m j", j=8)[:, tt * 64:(tt + 1) * 64, :],
                             in0=self.X[:, ck, ts].rearrange("p (m j) -> p m j", j=8),
                             scalar=self.gains[:, gidx * NCK + ck:gidx * NCK + ck + 1],
                             in1=self.RS[:, rs, :].rearrange("p (m j) -> p m j", j=8), op0=ALU.mult, op1=ALU.mult),
                         r=[f"X{ck}_{tt}", f"RS{rs}", "gains"], w=[f"XN{ck}_{t2}" for t2 in range(NTT)])
                elif out_bf16:
                    P.op("vector",
                         lambda e, ck=ck, ts=ts, rs=rs: e.scalar_tensor_tensor(
                             out=self.XN[:, ck, ts], in0=self.X[:, ck, ts],
                             scalar=self.gains[:, gidx * NCK + ck:gidx * NCK + ck + 1],
                             in1=self.RS[:, rs, :], op0=ALU.mult, op1=ALU.mult),
                         r=[f"X{ck}_{tt}", f"RS{rs}", "gains"], w=[f"XN{ck}_{tt}"])
                else:
                    P.op("vector",
                         lambda e, ck=ck, ts=ts, rs=rs: e.scalar_tensor_tensor(
                             out=self.X[:, ck, ts], in0=self.X[:, ck, ts],
                             scalar=self.gains[:, gidx * NCK + ck:gidx * NCK + ck + 1],
                             in1=self.RS[:, rs, :], op0=ALU.mult, op1=ALU.mult),
                         r=[f"X{ck}_{tt}", f"RS{rs}", "gains"], w=[f"X{ck}_{tt}"])

    def tail_norm(self, tt):
        if self.next_norm is not None:
            self.rmsnorm(tiles=[tt], **self.next_norm)
            if tt == NTT - 1:
                self.norm_done = True

    def ffn(self, l, i):
        P = self.P
        G = self.G
        own_norm = not self.norm_done
        self.norm_done = False
        if own_norm:
            self.rmsnorm(l * 2 + i, tiles=[0])
        c0 = 0
        while c0 < NFC:
            g_n = min(G, NFC - c0)
            s = self.rot("wslot", 2)
            P.op("gpsimd", lambda e, s=s, c0=c0, g_n=g_n: e.dma_start(
                out=self.W1S[:, s, :, :], in_=self.w1[l, i, c0 // G]),
                w=[f"W1S{s}"], dma=f"W1S{s}")
            P.op("gpsimd", lambda e, s=s, c0=c0, g_n=g_n: e.dma_start(
                out=self.W3S[:, s, :, :], in_=self.w3[l, i, c0 // G]),
                w=[f"W3S{s}"], dma=f"W3S{s}")
            P.op("gpsimd", lambda e, s=s, c0=c0, g_n=g_n: e.dma_start(
                out=self.W2S[:, s, :, :], in_=self.w2[l, i, c0 // G]),
                w=[f"W2S{s}"], dma=f"W2S{s}")
            for tt in range(NTT):
                ts = slice(tt * TT, (tt + 1) * TT)
                gs = self.rot("gt", 2)
                if own_norm and c0 == 0 and tt + 1 < NTT:
                    self.rmsnorm(l * 2 + i, tiles=[tt + 1])
                for g in range(g_n):
                    b1 = self.rot("b1", 2)
                    b3 = 2 + self.rot("b3", 2)

                    def up(e, wt, pb, g=g, s=s, ts=ts):
                        for ck in range(NCK):
                            ins = e.matmul(self.ps[pb][:, :], lhsT=wt[:, s, ck, g * 128:(g + 1) * 128],
                                           rhs=self.XN[:, ck, ts], start=(ck == 0), stop=(ck == NCK - 1))
                        return ins
                    xnr = [f"XN{ck}_{tt}" for ck in range(NCK)]
                    P.op("tensor", lambda e, up=up, b1=b1: up(e, self.W1S, b1), r=[f"W1S{s}"] + xnr, w=[f"ps{b1}"])
                    P.op("tensor", lambda e, up=up, b3=b3: up(e, self.W3S, b3), r=[f"W3S{s}"] + xnr, w=[f"ps{b3}"])
                    sg = self.rot("sg", 2)
                    P.op("scalar", lambda e, sg=sg, b1=b1: e.activation(out=self.SG[:, sg, :], in_=self.ps[b1][:, :],
                                                                        func=AF.Silu),
                         r=[f"ps{b1}"], w=[f"SG{sg}"])
                    P.op("vector", lambda e, sg=sg, b3=b3, gs=gs, g=g: e.tensor_tensor(
                        out=self.GT[:, gs, g, :], in0=self.SG[:, sg, :], in1=self.ps[b3][:, :], op=ALU.mult),
                        r=[f"SG{sg}", f"ps{b3}"], w=[f"GT{gs}_{g}"])
                if self.pending_down is not None:
                    self.pending_down()

                last_group = (c0 + g_n >= NFC)

                def emit_down(s=s, gs=gs, g_n=g_n, ts=ts, tt=tt, last_group=last_group):
                    for c in range(NCK):
                        bo = 4 + self.rot("bo", 3)

                        def down(e, c=c, bo=bo):
                            for g in range(g_n):
                                ins = e.matmul(self.ps[bo][:, :], lhsT=self.W2S[:, s, g, c * 128:(c + 1) * 128],
                                               rhs=self.GT[:, gs, g, :], start=(g == 0), stop=(g == g_n - 1))
                            return ins
                        P.op("tensor", down, r=[f"W2S{s}"] + [f"GT{gs}_{g}" for g in range(g_n)], w=[f"ps{bo}"])
                        P.op("vector", lambda e, c=c, bo=bo: e.scalar_tensor_tensor(
                            out=self.X[:, c, ts], in0=self.ps[bo][:, :], scalar=0.5, in1=self.X[:, c, ts],
                            op0=ALU.mult, op1=ALU.add),
                            r=[f"ps{bo}", f"X{c}_{tt}"], w=[f"X{c}_{tt}"])
                    if last_group:
                        self.tail_norm(tt)
                self.pending_down = emit_down
            c0 += g_n
        if self.pending_down is not None:
            self.pending_down()
            self.pending_down = None


    def even_prefetch(self):
        P = self.P
        COS = self.carve(11264, [S], BF16)
        SIN = self.carve(12288, [S], BF16)
        o = 18432
        MASK = self.carve(o, [4, 128], F32); o += 512
        GRET = self.carve(o, [4, 128], F32); o += 512
        SMALL = self.carve(o, [64], F32); o += 64
        o += 16
        IDENT = self.carve(o, [128], BF16); o += 64
        WG = self.carve(o, [2, 4, 128], BF16); o += 512

        def S_(e, out, in_):
            return e.dma_start(out=out, in_=in_)
        P.op("gpsimd", lambda e: S_(e, COS, self.d_cos), w=["COS"], dma="COS")
        P.op("gpsimd", lambda e: S_(e, SIN, self.d_sin), w=["SIN"], dma="SIN")
        P.op("sync", lambda e: S_(e, MASK, self.d_mask), w=["MASK"], dma="MASK")
        P.op("sync", lambda e: S_(e, GRET, self.d_gret), w=["GRET"], dma="GRET")
        P.op("sync", lambda e: S_(e, SMALL, self.d_small), w=["SMALL"], dma="SMALL")
        P.op("gpsimd", lambda e: S_(e, IDENT, self.d_ident), w=["IDENT"], dma="IDENT")
        P.op("gpsimd", lambda e: S_(e, WG, self.d_wg), w=["WG"], dma="WG")
        self.even_prefetched = True

    def even(self):
        P = self.P
        V = lambda fn, r, w: P.op("vector", fn, r=r, w=w)
        A = lambda fn, r, w: P.op("scalar", fn, r=r, w=w)
        T = lambda fn, r, w: P.op("tensor", fn, r=r, w=w)
        GP = lambda fn, r, w: P.op("gpsimd", fn, r=r, w=w)
        P.barrier()
        if not self.norm_done:
            self.rmsnorm(4)
        self.norm_done = False
        P.barrier()
        KT = self.carve(0, [16, 128], BF16)
        RB = self.carve(1024, [16, 128], BF16)
        MERGED = self.carve(3072, [NCK, S], BF16)
        COS = self.carve(11264, [S], BF16)
        SIN = self.carve(12288, [S], BF16)
        WM = self.carve(15360, [NCK, 768], BF16)
        o = 18432
        MASK = self.carve(o, [4, 128], F32); o += 512
        GRET = self.carve(o, [4, 128], F32); o += 512
        SMALL = self.carve(o, [64], F32); o += 64
        C12 = self.carve(o, [16], F32); o += 16
        IDENT = self.carve(o, [128], BF16); o += 64
        WG = self.carve(o, [2, 4, 128], BF16); o += 512
        assert o <= 20288
        o = 20288
        QP2 = self.carve(o, [2, S], BF16); o += 2048
        KP2 = self.carve(o, [2, S], BF16); o += 2048
        VT2 = self.carve(13312, [2, 16, 128], BF16)
        SGR2 = self.carve(o, [2, 16, 128], BF16); o += 2048
        TMP = self.carve(o, [3, 512], F32); o += 1536
        assert o <= self.AW
        o2 = 2048
        R = self.carve(o2, [128], F32); o2 += 128
        AS = self.carve(o2, [2, 128], BF16); o2 += 128
        MT = self.carve(o2, [4, 128], BF16); o2 += 256
        ST = self.carve(o2, [8, 4], F32); o2 += 32
        assert o2 <= 3072
        assert o <= self.AW
        o = 20288
        XL = self.carve(o, [2, 516], F32); o += 1032
        LW = self.carve(o, [9, 512], F32); o += 9 * 512
        LWB = self.carve(0, [6, 512], F32)
        XCB = self.carve(o, [512], BF16); o += 256
        HL = self.carve(o, [2, 512], F32); o += 1024
        assert o <= self.AW
        psb = [p[:, :].bitcast(BF16) for p in self.ps]

        if not self.even_prefetched:
            self.even_prefetch()
        self.even_prefetched = False

        def proj_fm(col0, tt, pb):
            ts = slice(tt * TT, (tt + 1) * TT)

            def f(e):
                for ck in range(NCK):
                    ins = e.matmul(self.ps[pb][:, :], lhsT=WM[:, ck, col0:col0 + 128], rhs=self.XN[:, ck, ts],
                                   start=(ck == 0), stop=(ck == NCK - 1))
                return ins
            T(f, ["WM"] + [f"XN{ck}_{tt}" for ck in range(NCK)], [f"ps{pb}"])

        A(lambda e: e.activation(out=C12[:, 0:4], in_=SMALL[:, 36:40], func=AF.Exp, scale=-1.0), ["SMALL"], ["C12a"])
        A(lambda e: e.activation(out=C12[:, 0:4], in_=C12[:, 0:4], func=AF.Ln, bias=1.0, scale=1.0), ["C12a"], ["C12a"])
        V(lambda e: e.tensor_scalar(out=C12[:, 4:8], in0=C12[:, 0:4], scalar1=-8.0, scalar2=None, op0=ALU.mult),
          ["C12a"], ["C12b"])
        V(lambda e: e.tensor_scalar(out=C12[:, 8:12], in0=C12[:, 0:4], scalar1=-16.0, scalar2=None, op0=ALU.mult),
          ["C12a"], ["C12c"])
        for b in range(4):
            P.op("gpsimd", lambda e, b=b: e.dma_start(out=WM[:, :, 0:256],
                                                     in_=self.d_wl[b].rearrange("(k p) n -> p k n", p=128)),
                 w=["WM"], dma="WM")
            for pair in range(2):
                ctx = []
                for tt in (2 * pair, 2 * pair + 1):
                    ts = slice(tt * TT, (tt + 1) * TT)
                    xs = tt % 2
                    par = tt % 2
                    LWs = [LW[:, k, :] for k in range(7)] if par == 0 else ([LWB[:, k, :] for k in range(6)] + [LW[:, 7, :]])
                    XCBs = XCB[:, :] if par == 0 else LW[:, 8, :].bitcast(BF16)[:, 0:512]
                    nm = lambda base, par=par: f"{base}{par}"
                    pxl = self.rot("psE", 8)
                    proj_fm(0, tt, pxl)
                    pgl = self.rot("psE", 8)
                    proj_fm(128, tt, pgl)
                    if tt == 0:
                        V(lambda e, xs=xs: e.memset(XL[:, xs, 0:4], 0.0), [], [f"XLh{xs}"])
                    A(lambda e, xs=xs, pxl=pxl: e.activation(out=XL[:, xs, 4:516], in_=self.ps[pxl][:, :], func=AF.Copy),
                      [f"ps{pxl}"], [f"XL{xs}"])
                    GP(lambda e, xs=xs: e.tensor_copy(out=XL[:, 1 - xs, 0:4], in_=XL[:, xs, 512:516]),
                       [f"XL{xs}"], [f"XLh{1 - xs}"])
                    GP(lambda e, LWs=LWs, xs=xs, b=b: e.tensor_scalar(out=LWs[0], in0=XL[:, xs, 4:516], scalar1=SMALL[:, 8 + b * 4 + 3:8 + b * 4 + 4],
                                                               scalar2=SMALL[:, 24 + b:25 + b], op0=ALU.mult, op1=ALU.add),
                       [f"XL{xs}", "SMALL"], [nm("XC")])
                    for k in range(3):
                        V(lambda e, xs=xs, b=b, k=k, LWs=LWs: e.scalar_tensor_tensor(
                            out=LWs[0], in0=XL[:, xs, 1 + k:513 + k], scalar=SMALL[:, 8 + b * 4 + k:8 + b * 4 + k + 1],
                            in1=LWs[0], op0=ALU.mult, op1=ALU.add),
                            [f"XL{xs}", f"XLh{xs}", "SMALL", nm("XC")], [nm("XC")])
                    ctx.append(dict(tt=tt, ts=ts, LWs=LWs, XCBs=XCBs, nm=nm, pgl=pgl, hs=tt % 2))
                for c in ctx:
                    A(lambda e, c=c: e.activation(out=c["XCBs"], in_=c["LWs"][0], func=AF.Copy), [c["nm"]("XC")], [c["nm"]("XCB")])
                for c in ctx:
                    pr = self.rot("psE", 8)
                    T(lambda e, c=c, pr=pr, b=b: e.matmul(self.ps[pr][:, :], lhsT=WG[:, 0, b, :], rhs=c["XCBs"], start=True, stop=True),
                      ["WG", c["nm"]("XCB")], [f"ps{pr}"])
                    pi = self.rot("psE", 8)
                    T(lambda e, c=c, pi=pi, b=b: e.matmul(self.ps[pi][:, :], lhsT=WG[:, 1, b, :], rhs=c["XCBs"], start=True, stop=True),
                      ["WG", c["nm"]("XCB")], [f"ps{pi}"])
                    c["pr"], c["pi"] = pr, pi
                for c in ctx:
                    A(lambda e, c=c, b=b: e.activation(out=c["LWs"][1], in_=self.ps[c["pr"]][:, :], func=AF.Sigmoid,
                                                      bias=SMALL[:, 28 + b:29 + b], scale=1.0), [f"ps{c['pr']}", "SMALL"], [c["nm"]("RR")])
                    A(lambda e, c=c, b=b: e.activation(out=c["LWs"][2], in_=self.ps[c["pi"]][:, :], func=AF.Sigmoid,
                                                      bias=SMALL[:, 32 + b:33 + b], scale=1.0), [f"ps{c['pi']}", "SMALL"], [c["nm"]("II")])
                for c in ctx:
                    A(lambda e, c=c, b=b: e.activation(out=c["LWs"][3], in_=c["LWs"][1], func=AF.Exp, scale=C12[:, 4 + b:5 + b]),
                      [c["nm"]("RR"), "C12b"], [c["nm"]("AA")])
                    A(lambda e, c=c, b=b: e.activation(out=c["LWs"][4], in_=c["LWs"][1], func=AF.Exp, scale=C12[:, 8 + b:9 + b]),
                      [c["nm"]("RR"), "C12c"], [c["nm"]("A2")])
                for c in ctx:
                    A(lambda e, c=c: e.activation(out=c["LWs"][4], in_=c["LWs"][4], func=AF.Sqrt, bias=1.0, scale=-1.0), [c["nm"]("A2")], [c["nm"]("A2")])
                for c in ctx:
                    nm_ = c["nm"]
                    V(lambda e, c=c: e.tensor_tensor(out=c["LWs"][5], in0=c["LWs"][4], in1=c["LWs"][2], op=ALU.mult), [nm_("A2"), nm_("II")], [nm_("BX")])
                    V(lambda e, c=c: e.tensor_tensor(out=c["LWs"][5], in0=c["LWs"][5], in1=c["LWs"][0], op=ALU.mult), [nm_("BX"), nm_("XC")], [nm_("BX")])
                    hs = c["hs"]
                    if c["tt"] == 0:
                        V(lambda e, c=c, hs=hs: e.tensor_tensor_scan(out=HL[:, hs, :], data0=c["LWs"][3], data1=c["LWs"][5], initial=0.0,
                                                                   op0=ALU.mult, op1=ALU.add), [nm_("AA"), nm_("BX")], [f"HL{hs}"])
                    else:
                        V(lambda e, c=c, hs=hs: e.tensor_tensor_scan(out=HL[:, hs, :], data0=c["LWs"][3], data1=c["LWs"][5],
                                                                   initial=HL[:, 1 - hs, 511:512], op0=ALU.mult, op1=ALU.add),
                          [nm_("AA"), nm_("BX"), f"HL{1 - hs}"], [f"HL{hs}"])
                for c in ctx:
                    A(lambda e, c=c: e.activation(out=c["LWs"][6], in_=self.ps[c["pgl"]][:, :], func=AF.Gelu_apprx_tanh),
                      [f"ps{c['pgl']}"], [c["nm"]("GG")])
                for c in ctx:
                    V(lambda e, c=c, b=b: e.tensor_tensor(out=MERGED[:, 4 + b, c["ts"]], in0=HL[:, c["hs"], :], in1=c["LWs"][6], op=ALU.mult),
                      [f"HL{c['hs']}", c["nm"]("GG")], [f"MG{4 + b}_{c['tt']}"])
        if self.lru_pending is not None:
            self.lru_pending()
            self.lru_pending = None
        P.barrier()

        gam = [1.0 - 2.0 ** (-5.0 - h) for h in range(4)]
        NT = 3

        def wm_load(h):
            return [lambda: P.op("gpsimd", lambda e: e.dma_start(out=WM[:, :, :], in_=self.d_wh[h].rearrange("(k p) n -> p k n", p=128)),
                                 w=["WM"], dma="WM")]

        def p_units(h):
            sl = h % 2
            units = []
            for (c0, dst, dn) in ((0, QP2[:, sl], "QP"), (256, KP2[:, sl], "KP")):
                for tt in range(NTT):
                    def u(c0=c0, dst=dst, dn=dn, tt=tt):
                        ts = slice(tt * TT, (tt + 1) * TT)
                        p0 = self.rot("psR", 6)
                        proj_fm(c0, tt, p0)
                        p1 = self.rot("psR", 6)
                        proj_fm(c0 + 128, tt, p1)
                        ta = self.rot("tmp", NT)
                        tb = self.rot("tmp", NT)
                        V(lambda e: e.tensor_tensor(out=TMP[:, ta, :], in0=self.ps[p0][:, :], in1=COS[:, ts], op=ALU.mult),
                          [f"ps{p0}", "COS"], [f"TMP{ta}"])
                        V(lambda e: e.tensor_tensor(out=TMP[:, tb, :], in0=self.ps[p1][:, :], in1=SIN[:, ts], op=ALU.mult),
                          [f"ps{p1}", "SIN"], [f"TMP{tb}"])
                        GP(lambda e: e.tensor_tensor(out=dst[:, ts], in0=TMP[:, ta, :], in1=TMP[:, tb, :], op=ALU.add),
                           [f"TMP{ta}", f"TMP{tb}"], [f"{dn}{sl}_{tt}"])
                    units.append(u)
            return units

        def vg_units(h):
            sl = h % 2
            VT, SGR = VT2[:, sl], SGR2[:, sl]
            units = []
            for blk in range(16):
                def u(blk=blk):
                    tt = blk // 4
                    pv = self.rot("psR", 6)

                    def fv(e):
                        for ck in range(NCK):
                            ins = e.matmul(self.ps[pv][:, 0:256], lhsT=self.XN[:, ck, blk * 128:(blk + 1) * 128], rhs=WM[:, ck, 512:768],
                                           start=(ck == 0), stop=(ck == NCK - 1))
                        return ins
                    T(fv, ["WM"] + [f"XN{ck}_{tt}" for ck in range(NCK)], [f"ps{pv}"])
                    A(lambda e: e.activation(out=VT[:, blk, :], in_=self.ps[pv][:, 0:128], func=AF.Copy), [f"ps{pv}"], [f"VT{sl}_{blk}"])
                    A(lambda e: e.activation(out=SGR[:, blk, :], in_=self.ps[pv][:, 128:256], func=AF.Silu), [f"ps{pv}"], [f"SGR{sl}_{blk}"])
                units.append(u)
            return units

        def ktr_units(h):
            sl = h % 2
            KP = KP2[:, sl]
            units = []
            for ng in range(4):
                def u(ng=ng):
                    pk = self.rot("psR", 6)

                    def fk(e):
                        for q in range(4):
                            n = ng * 4 + q
                            ins = e.transpose(out=psb[pk][:, q * 128:(q + 1) * 128], in_=KP[:, n * 128:(n + 1) * 128], identity=IDENT[:, :])
                        return ins
                    T(fk, ["IDENT", f"KP{sl}_{ng}"], [f"ps{pk}"])
                    A(lambda e: e.activation(out=KT[:, ng * 4:(ng + 1) * 4, :].rearrange("p a b -> p (a b)"),
                                             in_=psb[pk][:, 0:512], func=AF.Copy, scale=SMALL[:, 4 + h:5 + h]),
                      [f"ps{pk}", "SMALL"], [f"KT{ng}"])
                units.append(u)
            return units

        def rr_units(h):
            sl = h % 2
            VT = VT2[:, sl]
            c128 = gam[h] ** 128
            units = []

            def init():
                V(lambda e: e.memset(R[:, :], 0.0), [], ["R"])
                V(lambda e: e.memset(RB[:, 0, :], 0.0), [], ["RB0"])
            units.append(init)
            for n in range(15):
                def u(n=n):
                    pkv = self.rot("psR", 6)
                    T(lambda e: e.matmul(self.ps[pkv][:, 0:128], lhsT=KT[:, n, :], rhs=VT[:, n, :], start=True, stop=True),
                      [f"KT{n // 4}", f"VT{sl}_{n}"], [f"ps{pkv}"])
                    V(lambda e: e.scalar_tensor_tensor(out=R[:, :], in0=R[:, :], scalar=c128, in1=self.ps[pkv][:, 0:128],
                                                      op0=ALU.mult, op1=ALU.add), ["R", f"ps{pkv}"], ["R"])
                    A(lambda e: e.activation(out=RB[:, n + 1, :], in_=R[:, :], func=AF.Copy), ["R"], [f"RB{n + 1}"])
                units.append(u)
            return units

        def cl_units(h):
            sl = h % 2
            QP, KP, VT, SGR = QP2[:, sl], KP2[:, sl], VT2[:, sl], SGR2[:, sl]
            v3 = lambda ap: ap.rearrange("p (a b) -> p a b", b=128)
            bc = lambda ap: ap.unsqueeze(2).to_broadcast([128, 4, 128])
            units = []
            for ng in range(4):
                st8 = {}

                def start(ng=ng, st8=st8):
                    st8["po"] = 6 + self.rot("psO", 2)
                units.append(start)
                for q in range(4):
                    def u(ng=ng, q=q, st8=st8):
                        po = st8["po"]
                        n = ng * 4 + q
                        cs = slice(n * 128, (n + 1) * 128)
                        psc = self.rot("psR", 6)
                        T(lambda e: e.matmul(self.ps[psc][:, 0:128], lhsT=KP[:, cs], rhs=QP[:, cs], start=True, stop=True),
                          [f"KP{sl}_{ng}", f"QP{sl}_{ng}"], [f"ps{psc}"])
                        a_s = self.rot("as", 2)
                        V(lambda e: e.tensor_tensor(out=AS[:, a_s, :], in0=self.ps[psc][:, 0:128], in1=MASK[:, h, :], op=ALU.mult),
                          [f"ps{psc}", "MASK"], [f"AS{a_s}"])

                        def fo(e):
                            e.matmul(self.ps[po][:, q * 128:(q + 1) * 128], lhsT=AS[:, a_s, :], rhs=VT[:, n, :], start=True, stop=False)
                            return e.matmul(self.ps[po][:, q * 128:(q + 1) * 128], lhsT=QP[:, cs], rhs=RB[:, n, :], start=False, stop=True)
                        T(fo, [f"AS{a_s}", f"VT{sl}_{n}", f"QP{sl}_{ng}", f"RB{n}"], [f"ps{po}"])
                    units.append(u)

                def ln(ng=ng, st8=st8):
                    po = st8["po"]
                    t0 = self.rot("tmp", NT)
                    t1 = self.rot("tmp", NT)
                    t2 = self.rot("tmp", NT)
                    slt = self.rot("st", 2) * 4
                    A(lambda e: e.activation(out=TMP[:, t0, :], in_=self.ps[po][:, :], func=AF.Copy, scale=SMALL[:, h:h + 1]),
                      [f"ps{po}", "SMALL"], [f"TMP{t0}"])
                    V(lambda e: e.tensor_reduce(out=ST[:, slt, :], in_=v3(TMP[:, t0, :]), axis=AX.X, op=ALU.add), [f"TMP{t0}"], [f"ST{slt}"])
                    V(lambda e: e.tensor_scalar(out=ST[:, slt, :], in0=ST[:, slt, :], scalar1=-1.0 / 128, scalar2=None, op0=ALU.mult),
                      [f"ST{slt}"], [f"ST{slt}"])
                    V(lambda e: e.tensor_tensor(out=v3(TMP[:, t1, :]), in0=v3(TMP[:, t0, :]), in1=bc(ST[:, slt, :]), op=ALU.add),
                      [f"TMP{t0}", f"ST{slt}"], [f"TMP{t1}"])
                    GP(lambda e: e.tensor_tensor(out=TMP[:, t2, :], in0=TMP[:, t1, :], in1=TMP[:, t1, :], op=ALU.mult), [f"TMP{t1}"], [f"TMP{t2}"])
                    V(lambda e: e.tensor_reduce(out=ST[:, slt + 1, :], in_=v3(TMP[:, t2, :]), axis=AX.X, op=ALU.add), [f"TMP{t2}"], [f"ST{slt + 1}"])
                    A(lambda e: e.activation(out=ST[:, slt + 1, :], in_=ST[:, slt + 1, :], func=AF.Sqrt, bias=EPS, scale=1.0 / 128),
                      [f"ST{slt + 1}"], [f"ST{slt + 1}"])
                    V(lambda e: e.reciprocal(out=ST[:, slt + 1, :], in_=ST[:, slt + 1, :]), [f"ST{slt + 1}"], [f"ST{slt + 1}"])
                    V(lambda e: e.tensor_tensor(out=v3(TMP[:, t1, :]), in0=v3(TMP[:, t1, :]), in1=bc(ST[:, slt + 1, :]), op=ALU.mult),
                      [f"TMP{t1}", f"ST{slt + 1}"], [f"TMP{t1}"])
                    GP(lambda e: e.tensor_tensor(out=v3(TMP[:, t1, :]), in0=v3(TMP[:, t1, :]),
                                                 in1=GRET[:, h, :].unsqueeze(1).to_broadcast([128, 4, 128]), op=ALU.mult),
                       [f"TMP{t1}", "GRET"], [f"TMP{t1}"])
                    V(lambda e: e.tensor_tensor(out=MT[:, :, :], in0=v3(TMP[:, t1, :]), in1=SGR[:, ng * 4:(ng + 1) * 4, :], op=ALU.mult),
                      [f"TMP{t1}"] + [f"SGR{sl}_{ng * 4 + q}" for q in range(4)], ["MT"])
                    pt = self.rot("psR", 6)

                    def ft(e):
                        for q in range(4):
                            ins = e.transpose(out=psb[pt][:, q * 128:(q + 1) * 128], in_=MT[:, q, :], identity=IDENT[:, :])
                        return ins
                    T(ft, ["MT", "IDENT"], [f"ps{pt}"])
                    A(lambda e: e.activation(out=MERGED[:, h, ng * TT:(ng + 1) * TT], in_=psb[pt][:, 0:512], func=AF.Copy),
                      [f"ps{pt}"], [f"MG{h}_{ng}"])
                units.append(ln)
            return units

        for u in wm_load(0) + p_units(0) + vg_units(0) + ktr_units(0):
            u()
        for h in range(4):
            la = rr_units(h) + cl_units(h)
            lb = (wm_load(h + 1) + p_units(h + 1) + vg_units(h + 1)) if h + 1 < 4 else []
            for i in range(max(len(la), len(lb))):
                if i < len(la):
                    la[i]()
                if i < len(lb):
                    lb[i]()
            if h + 1 < 4:
                for u in ktr_units(h + 1):
                    u()
        P.barrier()
        for half in range(2):
            P.op("gpsimd", lambda e, half=half: e.dma_start(
                out=WM[:, :, 0:512], in_=self.d_wout.rearrange("(k p) n -> p k n", p=128)[:, :, half * 512:(half + 1) * 512]),
                w=["WM"], dma="WM")
            for tt in range(NTT):
                ts = slice(tt * TT, (tt + 1) * TT)
                for c4 in range(4):
                    c = half * 4 + c4
                    pw = self.rot("psE", 8)

                    def fw(e, c4=c4, ts=ts, pw=pw):
                        for k in range(NCK):
                            ins = e.matmul(self.ps[pw][:, :], lhsT=WM[:, k, c4 * 128:(c4 + 1) * 128], rhs=MERGED[:, k, ts],
                                           start=(k == 0), stop=(k == NCK - 1))
                        return ins
                    T(fw, ["WM"] + [f"MG{k}_{tt}" for k in range(NCK)], [f"ps{pw}"])
                    V(lambda e, c=c, ts=ts, pw=pw: e.tensor_tensor(out=self.X[:, c, ts], in0=self.ps[pw][:, :], in1=self.X[:, c, ts], op=ALU.add),
                      [f"ps{pw}", f"X{c}_{tt}"], [f"X{c}_{tt}"])
                if half == 1:
                    self.tail_norm(tt)
        P.barrier()


    def s5(self):
        P = self.P
        V = lambda fn, r, w: P.op("vector", fn, r=r, w=w)
        A = lambda fn, r, w: P.op("scalar", fn, r=r, w=w)
        T = lambda fn, r, w: P.op("tensor", fn, r=r, w=w)
        GP = lambda fn, r, w: P.op("gpsimd", fn, r=r, w=w)
        PI = math.pi
        TWO_PI = 2.0 * math.pi
        P.barrier()
        o = 3072
        RSALL = self.carve(o, [S], F32); o += S
        IOTA = self.carve(o, [260], F32); o += 260
        PM = self.carve(o, [4], F32); o += 4
        PP = self.carve(o, [3, 32], F32); o += 96
        BC = self.carve(o, [4, 32, 16], F32); o += 2048
        SC = self.carve(o, [12, 32], F32); o += 384
        PW = self.carve(o, [6, 32, 9], F32); o += 6 * 288
        BB = self.carve(o, [3, 32, 16], F32); o += 1536
        GD = self.carve(o, [8], F32); o += 8
        D8 = self.carve(o, [8], F32); o += 8
        ET = self.carve(o, [3, 4, 8, 16], F32); o += 1536
        E2 = self.carve(o, [8, 2, 128], BF16); o += 1024
        CL = self.carve(o, [3, 4, 9, 16], F32); o += 1728
        CT9_2 = self.carve(o, [2, 4, 9, 64], BF16); o += 2304
        BBS2 = self.carve(o, [2, 2, 128], BF16); o += 256
        ANG = self.carve(o, [5, 260], F32); o += 1300
        RF = self.carve(o, [256], F32); o += 256
        TM = self.carve(o, [4, 256], F32); o += 1024
        EE = self.carve(o, [2, 256], F32); o += 512
        STT = self.carve(o, [2, 260], F32); o += 520
        SB2 = self.carve(o, [2, 4, 2, 256], BF16); o += 2048
        SK = self.carve(o, [S], F32); o += S
        IDENT = self.carve(o, [128], BF16); o += 64
        assert o <= self.AW, o
        WGL = self.carve(5120, [2, NCK, 512], BF16)
        psb = [p[:, :].bitcast(BF16) for p in self.ps]

        def L(dst, src, key):
            P.op("sync", lambda e: e.dma_start(out=dst, in_=src), w=[key], dma="s5" + key)
        L(IOTA, self.d_iota, "IOTA")
        L(PM, self.d_pm, "PM")
        L(PP, self.d_s5pp, "PP")
        L(BC, self.d_s5bc, "BC")
        L(D8, self.d_s5d, "D8")
        P.op("gpsimd", lambda e: e.dma_start(out=IDENT, in_=self.d_ident), w=["IDENT5"], dma="IDENT5")
        lre, lim, ldt = PP[:, 0, :], PP[:, 1, :], PP[:, 2, :]
        DT, AA, TH, PHI, R8, NR, DEN, FR, FI, T0, T1 = [SC[:, k, :] for k in range(11)]
        GP(lambda e: e.tensor_tensor(out=GD[:, :], in0=self.gains[:, 40:48], in1=D8[:, :], op=ALU.mult), ["gains", "D8"], ["GD"])
        A(lambda e: e.activation(out=DT, in_=ldt, func=AF.Exp), ["PP"], ["DT"])
        GP(lambda e: e.tensor_tensor(out=AA, in0=lre, in1=DT, op=ALU.mult), ["PP", "DT"], ["AA"])
        GP(lambda e: e.tensor_tensor(out=TH, in0=lim, in1=DT, op=ALU.mult), ["PP", "DT"], ["TH"])
        MAGIC = 12582912.0
        INV2PI = 1.0 / TWO_PI
        SS = 0.99995

        def reduce_angle(dst, src, tmp, rd, wr):
            V(lambda e: e.tensor_scalar(out=tmp, in0=src, scalar1=INV2PI, scalar2=MAGIC, op0=ALU.mult, op1=ALU.add), rd, wr + ["_rt"])
            V(lambda e: e.tensor_scalar(out=tmp, in0=tmp, scalar1=-MAGIC, scalar2=-TWO_PI, op0=ALU.add, op1=ALU.mult), rd + ["_rt"], wr + ["_rt"])
            V(lambda e: e.tensor_tensor(out=dst, in0=src, in1=tmp, op=ALU.add), rd + ["_rt"], wr + ["_rt"])
        def red_pool0(dst, src, tmp, rd, wr):
            GP(lambda e: e.tensor_scalar(out=tmp, in0=src, scalar1=INV2PI, scalar2=MAGIC, op0=ALU.mult, op1=ALU.add), rd, wr + ["_rtq"])
            GP(lambda e: e.tensor_scalar(out=tmp, in0=tmp, scalar1=-MAGIC, scalar2=-TWO_PI, op0=ALU.add, op1=ALU.mult), rd + ["_rtq"], wr + ["_rtq"])
            GP(lambda e: e.tensor_tensor(out=dst, in0=src, in1=tmp, op=ALU.add), rd + ["_rtq"], wr + ["_rtq"])
        GP(lambda e: e.tensor_scalar(out=PHI, in0=TH, scalar1=8.0, scalar2=None, op0=ALU.mult), ["TH"], ["PHI"])
        red_pool0(PHI, PHI, T0, ["PHI"], ["PHI", "T0"])
        A(lambda e: e.activation(out=R8, in_=AA, func=AF.Exp, scale=8.0), ["AA"], ["R8"])
        i9 = IOTA[:, 0:9].unsqueeze(1).to_broadcast([128, 32, 9])
        bc9 = lambda ap: ap.unsqueeze(2).to_broadcast([128, 32, 9])
        GP(lambda e: e.tensor_tensor(out=PW[:, 0, :, :], in0=bc9(TH), in1=i9, op=ALU.mult), ["TH", "IOTA"], ["PW0"])
        GP(lambda e: e.tensor_tensor(out=PW[:, 2, :, :], in0=bc9(AA), in1=i9, op=ALU.mult), ["AA", "IOTA"], ["PW2"])
        A(lambda e: e.activation(out=PW[:, 2, :, :], in_=PW[:, 2, :, :], func=AF.Exp), ["PW2"], ["PW2"])
        GP(lambda e: e.tensor_scalar(out=PW[:, 1, :, :], in0=PW[:, 0, :, :], scalar1=0.5 * PI, scalar2=None, op0=ALU.add), ["PW0"], ["PW1"])
        red_pool0(PW[:, 0, :, :], PW[:, 0, :, :], PW[:, 5, :, :], ["PW0"], ["PW0", "PW5"])
        red_pool0(PW[:, 1, :, :], PW[:, 1, :, :], PW[:, 5, :, :], ["PW1"], ["PW1", "PW5"])
        A(lambda e: e.activation(out=PW[:, 0, :, :], in_=PW[:, 0, :, :], func=AF.Sin, scale=SS), ["PW0"], ["PW0"])
        A(lambda e: e.activation(out=PW[:, 1, :, :], in_=PW[:, 1, :, :], func=AF.Sin, scale=SS), ["PW1"], ["PW1"])
        GP(lambda e: e.tensor_tensor(out=PW[:, 3, :, :], in0=PW[:, 2, :, :], in1=PW[:, 1, :, :], op=ALU.mult), ["PW2", "PW1"], ["PWR"])
        GP(lambda e: e.tensor_tensor(out=PW[:, 4, :, :], in0=PW[:, 2, :, :], in1=PW[:, 0, :, :], op=ALU.mult), ["PW2", "PW0"], ["PWI"])
        PWR, PWI = PW[:, 3, :, :], PW[:, 4, :, :]
        GP(lambda e: e.tensor_scalar(out=NR, in0=PWR[:, :, 1], scalar1=-1.0, scalar2=None, op0=ALU.add), ["PWR"], ["NR"])
        NI = PWI[:, :, 1]
        GP(lambda e: e.tensor_tensor(out=DEN, in0=lre, in1=lre, op=ALU.mult), ["PP"], ["DEN"])
        GP(lambda e: e.tensor_tensor(out=T0, in0=lim, in1=lim, op=ALU.mult), ["PP"], ["T0"])
        GP(lambda e: e.tensor_tensor(out=DEN, in0=DEN, in1=T0, op=ALU.add), ["DEN", "T0"], ["DEN"])
        V(lambda e: e.reciprocal(out=DEN, in_=DEN), ["DEN"], ["DEN"])
        GP(lambda e: e.tensor_tensor(out=FR, in0=NR, in1=lre, op=ALU.mult), ["NR", "PP"], ["FR"])
        GP(lambda e: e.tensor_tensor(out=T0, in0=NI, in1=lim, op=ALU.mult), ["PWI", "PP", "DEN"], ["T0"])
        GP(lambda e: e.tensor_tensor(out=FR, in0=FR, in1=T0, op=ALU.add), ["FR", "T0"], ["FR"])
        GP(lambda e: e.tensor_tensor(out=FR, in0=FR, in1=DEN, op=ALU.mult), ["FR", "DEN"], ["FR"])
        GP(lambda e: e.tensor_tensor(out=FI, in0=NI, in1=lre, op=ALU.mult), ["PWI", "PP"], ["FI"])
        GP(lambda e: e.tensor_tensor(out=T1, in0=NR, in1=lim, op=ALU.mult), ["NR", "PP"], ["T1"])
        GP(lambda e: e.tensor_tensor(out=FI, in0=FI, in1=T1, op=ALU.subtract), ["FI", "T1"], ["FI"])
        GP(lambda e: e.tensor_tensor(out=FI, in0=FI, in1=DEN, op=ALU.mult), ["FI", "DEN"], ["FI"])
        bc16 = lambda ap: ap.unsqueeze(2).to_broadcast([128, 32, 16])
        bre, bim, cre, cim = BC[:, 0, :, :], BC[:, 1, :, :], BC[:, 2, :, :], BC[:, 3, :, :]
        BBR, BBI, BBT = BB[:, 0, :, :], BB[:, 1, :, :], BB[:, 2, :, :]
        GP(lambda e: e.tensor_tensor(out=BBR, in0=bre, in1=bc16(FR), op=ALU.mult), ["BC", "FR"], ["BBR"])
        GP(lambda e: e.tensor_tensor(out=BBT, in0=bim, in1=bc16(FI), op=ALU.mult), ["BC", "FI"], ["BBT"])
        GP(lambda e: e.tensor_tensor(out=BBR, in0=BBR, in1=BBT, op=ALU.subtract), ["BBR", "BBT"], ["BBR"])
        GP(lambda e: e.tensor_tensor(out=BBI, in0=bim, in1=bc16(FR), op=ALU.mult), ["BC", "FR"], ["BBI"])
        GP(lambda e: e.tensor_tensor(out=BBT, in0=bre, in1=bc16(FI), op=ALU.mult), ["BC", "FI", "BBR"], ["BBT"])
        GP(lambda e: e.tensor_tensor(out=BBI, in0=BBI, in1=BBT, op=ALU.add), ["BBI", "BBT"], ["BBI"])
        V(lambda e: e.memset(STT[:, :, 0:1], 0.0), [], ["STTz"])
        self.rmsnorm(5, rs_all=RSALL, deint=True)
        P.barrier()
        self.dbg("PWR", PW[:, 3, :, :], [128, 32, 9], F32, ["PWR"])
        self.dbg("PWI", PW[:, 4, :, :], [128, 32, 9], F32, ["PWI"])
        self.dbg("BBR", BBR, [128, 32, 16], F32, ["BBR"])
        self.dbg("BBI", BBI, [128, 32, 16], F32, ["BBI"])
        self.dbg("PHI", PHI, [128, 32], F32, ["PHI"])
        self.dbg("R8", R8, [128, 32], F32, ["R8"])


        BT2 = self.carve(0, [2, 8, 2, 128], BF16)
        KF2 = self.carve(2048, [2, 8, 128], BF16)
        A1, SN, CS, ATMP, A2 = [ANG[:, k, 0:257] for k in range(5)]

        def red_pool(dst, src, tmp, rd, wr):
            GP(lambda e: e.tensor_scalar(out=tmp, in0=src, scalar1=INV2PI, scalar2=MAGIC, op0=ALU.mult, op1=ALU.add), rd, wr + ["_rtp"])
            GP(lambda e: e.tensor_scalar(out=tmp, in0=tmp, scalar1=-MAGIC, scalar2=-TWO_PI, op0=ALU.add, op1=ALU.mult), rd + ["_rtp"], wr + ["_rtp"])
            GP(lambda e: e.tensor_tensor(out=dst, in0=src, in1=tmp, op=ALU.add), rd + ["_rtp"], wr + ["_rtp"])

        def tabE(ck):
            sl = ck % 2
            gs = slice(4 * ck, 4 * ck + 4)
            BT, KF, CT9, BBS, SB = BT2[:, sl], KF2[:, sl], CT9_2[:, sl], BBS2[:, sl], SB2[:, sl]
            bcE = lambda ap: ap.unsqueeze(2).to_broadcast([128, 4, 8, 16])
            pwE = lambda ap: ap.unsqueeze(3).to_broadcast([128, 4, 8, 16])
            Er, Ei, Et = ET[:, 0], ET[:, 1], ET[:, 2]
            V(lambda e: e.tensor_tensor(out=Er, in0=bcE(BBR[:, gs, :]), in1=pwE(PWR[:, gs, 0:8]), op=ALU.mult), ["BBR", "PWR"], ["Er"])
            V(lambda e: e.tensor_tensor(out=Et, in0=bcE(BBI[:, gs, :]), in1=pwE(PWI[:, gs, 0:8]), op=ALU.mult), ["BBI", "PWI"], ["Et"])
            V(lambda e: e.tensor_tensor(out=Er, in0=Er, in1=Et, op=ALU.subtract), ["Er", "Et"], ["Er"])
            V(lambda e: e.tensor_tensor(out=Ei, in0=bcE(BBR[:, gs, :]), in1=pwE(PWI[:, gs, 0:8]), op=ALU.mult), ["BBR", "PWI"], ["Ei"])
            V(lambda e: e.tensor_tensor(out=Et, in0=bcE(BBI[:, gs, :]), in1=pwE(PWR[:, gs, 0:8]), op=ALU.mult), ["BBI", "PWR", "Er"], ["Et"])
            V(lambda e: e.tensor_tensor(out=Ei, in0=Ei, in1=Et, op=ALU.add), ["Ei", "Et"], ["Ei"])
            E2v = E2.rearrange("p j r (g q c) -> p j r g q c", g=4, q=2)
            for ri, src, sn in ((0, Er, "Er"), (1, Ei, "Ei")):
                for q in range(2):
                    A(lambda e, ri=ri, src=src, q=q: e.activation(
                        out=E2v[:, :, ri, :, q, :], in_=src.rearrange("p g j c -> p j g c"), func=AF.Copy, scale=PM[:, q:q + 1]),
                        [sn, "PM"], [f"E2_{ri}"])
            for jq in range(4):
                pb = self.rot("psS", 2) + 6

                def ftr(e, jq=jq, pb=pb):
                    for u in range(4):
                        jp, ri = jq * 2 + u // 2, u % 2
                        ins = e.transpose(out=psb[pb][:, u * 128:(u + 1) * 128], in_=E2[:, jp, ri, :], identity=IDENT[:, :])
                    return ins
                T(ftr, ["E2_0", "E2_1", "IDENT5"], [f"ps{pb}"])
                A(lambda e, jq=jq, pb=pb: e.activation(out=BT[:, jq * 2:jq * 2 + 2, :, :].rearrange("p a b c -> p (a b c)"),
                                                       in_=psb[pb][:, 0:512], func=AF.Copy), [f"ps{pb}"], [f"BT{sl}_{jq}"])
            bcC = lambda ap: ap.unsqueeze(2).to_broadcast([128, 4, 9, 16])
            pwC = lambda ap: ap.unsqueeze(3).to_broadcast([128, 4, 9, 16])
            Cr, Ci, Ct = CL[:, 0], CL[:, 1], CL[:, 2]
            GP(lambda e: e.tensor_tensor(out=Cr, in0=bcC(cre[:, gs, :]), in1=pwC(PWR[:, gs, :]), op=ALU.mult), ["BC", "PWR"], ["Cr"])
            GP(lambda e: e.tensor_tensor(out=Ct, in0=bcC(cim[:, gs, :]), in1=pwC(PWI[:, gs, :]), op=ALU.mult), ["BC", "PWI"], ["Ct"])
            GP(lambda e: e.tensor_tensor(out=Cr, in0=Cr, in1=Ct, op=ALU.subtract), ["Cr", "Ct"], ["Cr"])
            GP(lambda e: e.tensor_tensor(out=Ci, in0=bcC(cre[:, gs, :]), in1=pwC(PWI[:, gs, :]), op=ALU.mult), ["BC", "PWI"], ["Ci"])
            GP(lambda e: e.tensor_tensor(out=Ct, in0=bcC(cim[:, gs, :]), in1=pwC(PWR[:, gs, :]), op=ALU.mult), ["BC", "PWR", "Cr"], ["Ct"])
            GP(lambda e: e.tensor_tensor(out=Ci, in0=Ci, in1=Ct, op=ALU.add), ["Ci", "Ct"], ["Ci"])
            CTv = CT9.rearrange("p g k (r q c) -> p g k r q c", r=2, q=2)
            for ri, src, sn in ((0, Cr, "Cr"), (1, Ci, "Ci")):
                for q in range(2):
                    A(lambda e, ri=ri, src=src, q=q: e.activation(
                        out=CTv[:, :, :, ri, q, :], in_=src, func=AF.Copy, scale=PM[:, 2 * ri + q:2 * ri + q + 1]),
                        [sn, "PM"], [f"CT9_{sl}"])
            BBv = BBS.rearrange("p r (g q c) -> p r g q c", g=4, q=2)
            for ri, src, sn in ((0, BBR, "BBR"), (1, BBI, "BBI")):
                for q in range(2):
                    A(lambda e, ri=ri, src=src, q=q: e.activation(
                        out=BBv[:, ri, :, q, :], in_=src[:, gs, :], func=AF.Copy, scale=PM[:, q:q + 1]),
                        [sn, "PM"], [f"BBS{sl}"])
            for hf in range(2):
                pk = 4 + hf

                def fkf(e, hf=hf, pk=pk):
                    for g4 in range(4):
                        for t4 in range(4):
                            tau = hf * 4 + t4
                            outv = self.ps[pk][32 * g4:32 * g4 + 32, t4 * 128 + 32 * g4:t4 * 128 + 32 * g4 + 32]
                            e.matmul(outv, lhsT=BBS[:, 0, 32 * g4:32 * g4 + 32], rhs=CT9[:, g4, tau, 0:32],
                                     start=True, stop=False, tile_position=(0, 32 * g4))
                            ins = e.matmul(outv, lhsT=BBS[:, 1, 32 * g4:32 * g4 + 32], rhs=CT9[:, g4, tau, 32:64],
                                           start=False, stop=True, tile_position=(0, 32 * g4))
                    return ins
                T(fkf, [f"BBS{sl}", f"CT9_{sl}"], [f"ps{pk}"])
                A(lambda e, hf=hf, pk=pk: e.activation(out=KF[:, hf * 4:hf * 4 + 4, :].rearrange("p a b -> p (a b)"), in_=self.ps[pk][:, :], func=AF.Copy),
                  [f"ps{pk}"], [f"KF{sl}_{hf}"])
            xn_all = [f"XN{ck}_{tt}" for tt in range(NTT)]
            uv = self.XN[:, ck, :].rearrange("p (j m) -> p j m", j=8)
            for g4 in range(4):
                gp = 4 * ck + g4
                pe = self.rot("psS", 2) + 6

                def fe(e, g4=g4, pe=pe):
                    for ri in range(2):
                        for j in range(8):
                            ins = e.matmul(self.ps[pe][:, ri * 256:(ri + 1) * 256], lhsT=BT[32 * g4:32 * g4 + 32, 7 - j, ri, :],
                                           rhs=uv[32 * g4:32 * g4 + 32, j, :], start=(j == 0), stop=(j == 7),
                                           tile_position=(32 * g4, 0))
                    return ins
                T(fe, [f"BT{sl}_{q}" for q in range(4)] + xn_all, [f"ps{pe}"])
                V(lambda e, gp=gp: e.tensor_scalar(out=A1, in0=IOTA[:, 0:257], scalar1=PHI[:, gp:gp + 1], scalar2=None, op0=ALU.mult),
                  ["IOTA", "PHI"], ["A1"])
                red_pool(SN, A1, ATMP, ["A1"], ["SN", "ATMP"])
                V(lambda e: e.tensor_scalar(out=A2, in0=A1, scalar1=0.5 * PI, scalar2=None, op0=ALU.add), ["A1"], ["A2"])
                reduce_angle(CS, A2, CS, ["A2"], ["CS"])
                A(lambda e: e.activation(out=SN, in_=SN, func=AF.Sin, scale=SS), ["SN"], ["SN"])
                A(lambda e: e.activation(out=CS, in_=CS, func=AF.Sin, scale=SS), ["CS"], ["CS"])
                GP(lambda e, gp=gp: e.tensor_scalar(out=RF[:, :], in0=IOTA[:, 0:256], scalar1=0.0, scalar2=R8[:, gp:gp + 1], op0=ALU.mult, op1=ALU.add),
                   ["IOTA", "R8"], ["RF"])
                ere, eim = self.ps[pe][:, 0:256], self.ps[pe][:, 256:512]
                V(lambda e, ere=ere: e.tensor_tensor(out=TM[:, 0, :], in0=ere, in1=CS[:, 1:257], op=ALU.mult), [f"ps{pe}", "CS"], ["TM0"])
                V(lambda e, eim=eim: e.tensor_tensor(out=TM[:, 1, :], in0=eim, in1=SN[:, 1:257], op=ALU.mult), [f"ps{pe}", "SN"], ["TM1"])
                V(lambda e, eim=eim: e.tensor_tensor(out=TM[:, 2, :], in0=eim, in1=CS[:, 1:257], op=ALU.mult), [f"ps{pe}", "CS"], ["TM2"])
                V(lambda e, ere=ere: e.tensor_tensor(out=TM[:, 3, :], in0=ere, in1=SN[:, 1:257], op=ALU.mult), [f"ps{pe}", "SN"], ["TM3"])
                GP(lambda e: e.tensor_tensor(out=EE[:, 0, :], in0=TM[:, 0, :], in1=TM[:, 1, :], op=ALU.add), ["TM0", "TM1"], ["EE0"])
                GP(lambda e: e.tensor_tensor(out=EE[:, 1, :], in0=TM[:, 2, :], in1=TM[:, 3, :], op=ALU.subtract), ["TM2", "TM3"], ["EE1"])
                for ri in range(2):
                    V(lambda e, ri=ri: e.tensor_tensor_scan(out=STT[:, ri, 1:257], data0=RF[:, :], data1=EE[:, ri, :], initial=0.0,
                                                           op0=ALU.mult, op1=ALU.add), ["RF", f"EE{ri}", "STTz"], [f"STT{ri}"])
                V(lambda e: e.tensor_tensor(out=TM[:, 0, :], in0=STT[:, 0, 0:256], in1=CS[:, 0:256], op=ALU.mult), ["STT0", "CS"], ["TM0"])
                V(lambda e: e.tensor_tensor(out=TM[:, 1, :], in0=STT[:, 1, 0:256], in1=SN[:, 0:256], op=ALU.mult), ["STT1", "SN"], ["TM1"])
                GP(lambda e: e.tensor_tensor(out=TM[:, 2, :], in0=STT[:, 0, 0:256], in1=SN[:, 0:256], op=ALU.mult), ["STT0", "SN"], ["TM2"])
                GP(lambda e: e.tensor_tensor(out=TM[:, 3, :], in0=STT[:, 1, 0:256], in1=CS[:, 0:256], op=ALU.mult), ["STT1", "CS"], ["TM3"])
                GP(lambda e, g4=g4: e.tensor_tensor(out=SB[:, g4, 0, :], in0=TM[:, 0, :], in1=TM[:, 1, :], op=ALU.subtract), ["TM0", "TM1"], [f"SB{sl}_{g4}"])
                GP(lambda e, g4=g4: e.tensor_tensor(out=SB[:, g4, 1, :], in0=TM[:, 2, :], in1=TM[:, 3, :], op=ALU.add), ["TM2", "TM3"], [f"SB{sl}_{g4}"])

        def ypart(ck):
            sl = ck % 2
            KF, CT9, SB = KF2[:, sl], CT9_2[:, sl], SB2[:, sl]
            xn_all = [f"XN{ck}_{tt}" for tt in range(NTT)]
            uv = self.XN[:, ck, :].rearrange("p (j m) -> p j m", j=8)
            for i in range(8):
                pb = i // 2
                reg = self.ps[pb][:, (i % 2) * 256:(i % 2 + 1) * 256]

                def fy(e, i=i, pb=pb, reg=reg):
                    for j in range(i + 1):
                        e.matmul(reg, lhsT=KF[:, i - j, :], rhs=uv[:, j, :], start=(j == 0), stop=False)
                    for g4 in range(4):
                        for ri in range(2):
                            ins = e.matmul(self.ps[pb][32 * g4:32 * g4 + 32, (i % 2) * 256:(i % 2 + 1) * 256],
                                           lhsT=CT9[:, g4, i + 1, ri * 32:(ri + 1) * 32], rhs=SB[:, g4, ri, :],
                                           start=False, stop=(g4 == 3 and ri == 1), tile_position=(0, 32 * g4))
                    return ins
                T(fy, [f"KF{sl}_0", f"KF{sl}_1", f"CT9_{sl}"] + [f"SB{sl}_{q}" for q in range(4)] + xn_all, [f"ps{pb}"])

        def evac(ck):
            for tt in range(NTT):
                ts = slice(tt * TT, (tt + 1) * TT)
                V(lambda e, ts=ts: e.scalar_tensor_tensor(out=SK[:, ts], in0=self.X[:, ck, ts], scalar=GD[:, ck:ck + 1], in1=RSALL[:, ts],
                                                         op0=ALU.mult, op1=ALU.mult), [f"X{ck}_{tt}", "GD", "RSALL"], [f"SK{tt}"])
            SKv = SK.rearrange("p (m i) -> p i m", i=8)
            for pb in range(4):
                V(lambda e, pb=pb: e.tensor_tensor(out=SKv[:, 2 * pb:2 * pb + 2, :], in0=self.ps[pb][:, :].rearrange("p (i m) -> p i m", i=2),
                                                   in1=SKv[:, 2 * pb:2 * pb + 2, :], op=ALU.add),
                  [f"ps{pb}"] + [f"SK{tt}" for tt in range(NTT)], [f"SK{tt}" for tt in range(NTT)])
            for tt in range(NTT):
                ts = slice(tt * TT, (tt + 1) * TT)
                A(lambda e, ts=ts: e.activation(out=self.XN[:, ck, ts], in_=SK[:, ts], func=AF.Gelu_apprx_tanh),
                  [f"SK{tt}"], [f"XN{ck}_{tt}"])

        for pk in (4, 5):
            V(lambda e, pk=pk: e.memset(self.ps[pk][:, :], 0.0), [], [f"ps{pk}"])
        tabE(0)
        for ck in range(NCK):
            ypart(ck)
            if ck + 1 < NCK:
                tabE(ck + 1)
            evac(ck)

        P.barrier()
        for half in range(2):
            for ab in range(2):
                src = self.d_glu[ab].rearrange("(k p) n -> p k n", p=128)[:, :, half * 512:(half + 1) * 512]
                P.op("gpsimd", lambda e, ab=ab, src=src: e.dma_start(out=WGL[:, ab, :, :], in_=src), w=[f"WGL{ab}"], dma=f"WGL{ab}")
            for tt in range(NTT):
                ts = slice(tt * TT, (tt + 1) * TT)
                for c4 in range(4):
                    c = half * 4 + c4
                    pa = self.rot("psG", 4)
                    pb_ = 4 + self.rot("psG2", 4)

                    def fg(e, ab, pbank, c4=c4, ts=ts):
                        for k in range(NCK):
                            ins = e.matmul(self.ps[pbank][:, :], lhsT=WGL[:, ab, k, c4 * 128:(c4 + 1) * 128], rhs=self.XN[:, k, ts],
                                           start=(k == 0), stop=(k == NCK - 1))
                        return ins
                    zr = [f"XN{k}_{tt}" for k in range(NCK)]
                    T(lambda e, fg=fg, pa=pa: fg(e, 0, pa), ["WGL0"] + zr, [f"ps{pa}"])
                    T(lambda e, fg=fg, pb_=pb_: fg(e, 1, pb_), ["WGL1"] + zr, [f"ps{pb_}"])
                    sg = self.rot("tm5", 2)
                    A(lambda e, sg=sg, pb_=pb_: e.activation(out=SK[:, sg * 512:(sg + 1) * 512], in_=self.ps[pb_][:, :], func=AF.Sigmoid),
                      [f"ps{pb_}"], [f"SK{sg}"])
                    V(lambda e, sg=sg, pa=pa: e.tensor_tensor(out=SK[:, sg * 512:(sg + 1) * 512], in0=self.ps[pa][:, :], in1=SK[:, sg * 512:(sg + 1) * 512], op=ALU.mult),
                      [f"ps{pa}", f"SK{sg}"], [f"SK{sg}"])
                    GP(lambda e, sg=sg, c=c, ts=ts: e.tensor_tensor(out=self.X[:, c, ts], in0=self.X[:, c, ts], in1=SK[:, sg * 512:(sg + 1) * 512], op=ALU.add),
                       [f"SK{sg}", f"X{c}_{tt}"], [f"X{c}_{tt}"])
                if half == 1:
                    self.tail_norm(tt)
        P.barrier()

    def final(self, sq):
        P = self.P
        if not self.norm_done:
            self.rmsnorm(6, out_bf16=False)
        self.norm_done = False
        for ck in range(NCK):
            P.op("sync",
                 lambda e, ck=ck: e.dma_start(out=self.outT[sq, ck * 128:(ck + 1) * 128, :], in_=self.X[:, ck, :]),
                 r=[f"X{ck}_{tt}" for tt in range(NTT)], dma="out")

    def store_x(self, sq):
        P = self.P
        for ck in range(NCK):
            P.op("sync",
                 lambda e, ck=ck: e.dma_start(out=self.outT[sq, ck * 128:(ck + 1) * 128, :], in_=self.X[:, ck, :]),
                 r=[f"X{ck}_{tt}" for tt in range(NTT)], dma="out")

    def build(self):
        st = self.stages
        self.setup()
        for sq in range(self.nseq):
            self.load_x(sq)
            for si, name in enumerate(st):
                nxt = st[si + 1] if si + 1 < len(st) else None
                self.next_norm = None
                if nxt is not None and HOIST_NORM:
                    if nxt.startswith("ffn"):
                        self.next_norm = dict(gidx=int(nxt[3]) * 2 + int(nxt[4]))
                    elif nxt == "even":
                        self.next_norm = dict(gidx=4)
                    elif nxt == "final":
                        self.next_norm = dict(gidx=6, out_bf16=False)
                if nxt == "even" and name.startswith("ffn"):
                    self.even_prefetch()
                if name == "final":
                    self.final(sq)
                elif name == "store":
                    self.store_x(sq)
                elif name == "even":
                    self.even()
                elif name == "s5":
                    self.s5()
                elif name.startswith("ffn"):
                    self.ffn(int(name[3]), int(name[4]))
                else:
                    raise ValueError(name)
        self.P.emit(["out"])
        self.es.close()
        return self.nc


ALL_STAGES = ["ffn00", "even", "ffn01", "ffn10", "s5", "ffn11", "final"]


def prep_inputs(inputs, b0, nseq):
    x = inputs["x"]
    m = {}
    m["xT"] = np.ascontiguousarray(np.transpose(x[b0:b0 + nseq], (0, 2, 1)))
    G0 = FFN_G
    ng = NFC // G0
    m["ffn_w1"] = np.ascontiguousarray(inputs["ffn_w1"].reshape(2, 2, NCK, 128, ng, G0 * 128).transpose(0, 1, 4, 3, 2, 5))
    m["ffn_w3"] = np.ascontiguousarray(inputs["ffn_w3"].reshape(2, 2, NCK, 128, ng, G0 * 128).transpose(0, 1, 4, 3, 2, 5))
    m["ffn_w2"] = np.ascontiguousarray(inputs["ffn_w2"].reshape(2, 2, ng, G0, 128, D).transpose(0, 1, 2, 4, 3, 5))
    g = np.concatenate([inputs["ffn_norm_g"].reshape(4, D), inputs["mix_norm_g"].reshape(2, D),
                        inputs["final_norm_g"].reshape(1, D)], axis=0)
    m["gains"] = np.ascontiguousarray(g.reshape(7, NCK, 128).transpose(2, 0, 1).reshape(128, 7 * NCK))
    m.update(even_consts())
    w_in = inputs["w_in_even"][0]
    perm = np.concatenate([np.arange(64, 128), np.arange(0, 64)])
    wh = np.empty((4, D, 768), np.float32)
    for h in range(4):
        q = w_in[:, h * 128:(h + 1) * 128]
        k = w_in[:, 512 + h * 128:512 + (h + 1) * 128]
        wh[h, :, 0:128] = q
        wh[h, :, 128:256] = q[:, perm]
        wh[h, :, 256:384] = k
        wh[h, :, 384:512] = k[:, perm]
        wh[h, :, 512:640] = w_in[:, 1024 + h * 128:1024 + (h + 1) * 128]
        wh[h, :, 640:768] = w_in[:, 1536 + h * 128:1536 + (h + 1) * 128]
    m["w_head"] = wh
    wl = np.empty((4, D, 256), np.float32)
    for b in range(4):
        wl[b, :, 0:128] = w_in[:, 2048 + b * 128:2048 + (b + 1) * 128]
        wl[b, :, 128:256] = w_in[:, 2560 + b * 128:2560 + (b + 1) * 128]
    m["w_lru"] = wl
    m["w_out_even"] = inputs["w_out_even"][0]
    m["gret"] = np.ascontiguousarray(np.broadcast_to(inputs["ret_norm_g"][0].reshape(1, 4, 128), (128, 4, 128)))
    small = m["even_small"]
    small[:, 8:24] = inputs["conv_w"][0].reshape(4, 4, 128).transpose(2, 1, 0).reshape(128, 16)
    small[:, 24:28] = inputs["conv_b"][0].reshape(4, 128).T
    small[:, 28:32] = inputs["lru_b_a"][0].reshape(4, 128).T
    small[:, 32:36] = inputs["lru_b_i"][0].reshape(4, 128).T
    small[:, 36:40] = inputs["lru_lambda"][0].reshape(4, 128).T
    def gp_layout(a):
        sh = a.shape[2:]
        return np.ascontiguousarray(a.reshape((32, 2, 64) + sh).transpose((1, 2, 0) + tuple(range(3, 3 + len(sh)))).reshape((128, 32) + sh))
    lre = gp_layout(inputs["s5_lambda_re"][0])
    lim = gp_layout(inputs["s5_lambda_im"][0])
    ldt = gp_layout(np.ascontiguousarray(np.broadcast_to(inputs["s5_log_dt"][0][:, None], (64, 64))))
    m["s5_pp"] = np.ascontiguousarray(np.stack([lre, lim, ldt], 1))
    bre = gp_layout(inputs["s5_b_re"][0])
    bim = gp_layout(inputs["s5_b_im"][0])
    cre = gp_layout(np.ascontiguousarray(inputs["s5_c_re"][0].transpose(0, 2, 1)))
    cim = gp_layout(np.ascontiguousarray(inputs["s5_c_im"][0].transpose(0, 2, 1)))
    m["s5_bc"] = np.ascontiguousarray(np.stack([bre, bim, cre, cim], 1))
    m["s5_d8"] = np.ascontiguousarray(inputs["s5_d"][0].reshape(8, 128).T)
    m["glu_w"] = np.ascontiguousarray(np.stack([inputs["glu_w_a"][0], inputs["glu_w_b"][0]], 0))
    m["lru_wg"] = np.ascontiguousarray(np.stack([inputs["lru_w_a"][0], inputs["lru_w_i"][0]], 0).transpose(2, 0, 1, 3))
    return m


def even_consts():
    half = 64
    inv = (np.float32(10000.0) ** (-np.arange(half, dtype=np.float32) / np.float32(half))).astype(np.float32)
    ang = (np.arange(S, dtype=np.float32)[:, None] * inv[None, :]).astype(np.float32)
    cos = np.cos(ang).astype(np.float32).T
    sin = np.sin(ang).astype(np.float32).T
    c = {}
    c["rope_cos"] = np.ascontiguousarray(np.concatenate([cos, cos], 0))
    c["rope_sin"] = np.ascontiguousarray(np.concatenate([-sin, sin], 0))
    gam = 1.0 - 2.0 ** (-5.0 - np.arange(4, dtype=np.float64))
    j = np.arange(128, dtype=np.float64)
    mask = np.zeros((128, 4, 128), np.float64)
    for h in range(4):
        mask[:, h, :] = (128.0 ** -0.5) * (gam[h] ** (-(j[:, None] + 1.0))) * (j[None, :] >= j[:, None])
    c["ret_mask"] = mask.astype(np.float32)
    small = np.zeros((128, 64), np.float32)
    for h in range(4):
        small[:, h] = gam[h] ** (j + 1.0)
        small[:, 4 + h] = (128.0 ** -0.5) * gam[h] ** (127.0 - j)
    c["even_small"] = small
    c["ident"] = np.eye(128, dtype=np.float32)
    c["iota"] = np.ascontiguousarray(np.broadcast_to(np.arange(260, dtype=np.float32)[None, :], (128, 260)))
    pm = np.zeros((128, 4), np.float32)
    pm[0:64, 0] = 1.0
    pm[64:128, 1] = 1.0
    pm[:, 2:4] = -pm[:, 0:2]
    c["parmask"] = pm
    return c


def kernel(**inputs):
    inputs = {k: np.asarray(v) for k, v in inputs.items()}
    ncores = 8
    nc = Builder(stages=ALL_STAGES).build()
    in_maps = [prep_inputs(inputs, c * NSEQ, NSEQ) for c in range(ncores)]
    res = run_bass_kernel_spmd(nc, in_maps, core_ids=list(range(ncores)))
    outs = [np.transpose(r["outT"], (0, 2, 1)) for r in res.results]
    return np.ascontiguousarray(np.concatenate(outs, axis=0)).astype(np.float32)
```
